# Optimizing a Trainium2 kernel written in Bass

```python
import jax, jax.numpy as jnp
from jax import lax
import numpy as np

D_MODEL = 2048
BATCH = 2
SEQ = 8192
DEPTH = 1

CHUNK = 128
A_HEADS = 8
A_WIDTH = D_MODEL
A_HEAD_DIM = A_WIDTH // A_HEADS
B_GROUPS = 16
B_WIDTH = D_MODEL
CONV_WIDTH = 3
N_BRANCH = 2
DN_ALPHA = (2.0 * DEPTH) ** 0.25
DN_BETA = (8.0 * DEPTH) ** -0.25
LN_EPS = 1e-5
IN_WIDTHS = (A_WIDTH, A_WIDTH, A_WIDTH, B_WIDTH, B_WIDTH, B_WIDTH, B_WIDTH, D_MODEL, D_MODEL)
IN_COLS = 3 * A_WIDTH + 4 * B_WIDTH + N_BRANCH * D_MODEL

kernel_name = "hybrid_gated_sgu_shortconv_deepnorm"


def _split_points():
    pts, acc = [], 0
    for w in IN_WIDTHS[:-1]:
        acc += w
        pts.append(acc)
    return pts


def layer_norm(x, g, b):
    xf = x.astype(jnp.float32)
    mu = jnp.mean(xf, axis=-1, keepdims=True)
    var = jnp.mean(jnp.square(xf - mu), axis=-1, keepdims=True)
    y = (xf - mu) * lax.rsqrt(var + LN_EPS)
    return (y * g.astype(jnp.float32) + b.astype(jnp.float32)).astype(x.dtype)


def chunked_sgu(u, v, ln_g, ln_b, w_s, b_s):
    bsz, s, _ = u.shape
    nc = s // CHUNK
    u = jax.nn.gelu(u).reshape(bsz, nc, CHUNK, A_HEADS, A_HEAD_DIM)
    v = jax.nn.gelu(v).reshape(bsz, nc, CHUNK, A_HEADS, A_HEAD_DIM)
    v = layer_norm(v, ln_g.reshape(A_HEADS, A_HEAD_DIM), ln_b.reshape(A_HEADS, A_HEAD_DIM))
    causal = jnp.tril(jnp.ones((CHUNK, CHUNK), dtype=bool))
    w = jnp.where(causal[None], w_s, jnp.zeros_like(w_s))
    mixed = jnp.einsum('hts,bcshd->bcthd', w, v)
    mixed = mixed + jnp.transpose(b_s)[None, None, :, :, None]
    return (u * mixed).reshape(bsz, s, A_WIDTH)


def short_gated_conv(xb, cb, bb, conv_w, conv_b):
    s = xb.shape[1]
    h = cb * xb
    hp = jnp.pad(h, ((0, 0), (CONV_WIDTH - 1, 0), (0, 0)))
    conv = conv_b + conv_w[0] * hp[:, 0:s, :]
    for k in range(1, CONV_WIDTH):
        conv = conv + conv_w[k] * hp[:, k:k + s, :]
    return bb * conv


def setup_inputs(seed: int = 0) -> dict:
    key = jax.random.key(seed)
    ks = jax.random.split(key, 16)
    nrm = jax.random.normal
    x = nrm(ks[0], (BATCH, SEQ, D_MODEL), jnp.float32)
    w_in = nrm(ks[1], (DEPTH, D_MODEL, IN_COLS), jnp.float32) * D_MODEL ** -0.5
    b_gate = 0.02 * nrm(ks[2], (DEPTH, N_BRANCH * D_MODEL), jnp.float32)
    ln_v_g = 1.0 + 0.02 * nrm(ks[3], (DEPTH, A_WIDTH), jnp.float32)
    ln_v_b = 0.02 * nrm(ks[4], (DEPTH, A_WIDTH), jnp.float32)
    w_s = nrm(ks[5], (DEPTH, A_HEADS, CHUNK, CHUNK), jnp.float32) * (0.5 * CHUNK ** -0.5)
    b_s = 1.0 + 0.02 * nrm(ks[6], (DEPTH, A_HEADS, CHUNK), jnp.float32)
    conv_w = nrm(ks[7], (DEPTH, CONV_WIDTH, B_WIDTH), jnp.float32) * CONV_WIDTH ** -0.5
    conv_b = 0.02 * nrm(ks[8], (DEPTH, B_WIDTH), jnp.float32)
    w_oa = nrm(ks[9], (DEPTH, A_WIDTH, D_MODEL), jnp.float32) * (A_WIDTH ** -0.5 * DN_BETA)
    w_ob = nrm(ks[10], (DEPTH, B_WIDTH, D_MODEL), jnp.float32) * (B_WIDTH ** -0.5 * DN_BETA)
    w_out = nrm(ks[11], (DEPTH, D_MODEL, D_MODEL), jnp.float32) * (D_MODEL ** -0.5 * DN_BETA)
    ln_g = 1.0 + 0.02 * nrm(ks[12], (DEPTH, D_MODEL), jnp.float32)
    ln_b = 0.02 * nrm(ks[13], (DEPTH, D_MODEL), jnp.float32)
    return {"x": x, "w_in": w_in, "b_gate": b_gate, "ln_v_g": ln_v_g, "ln_v_b": ln_v_b,
            "w_s": w_s, "b_s": b_s, "conv_w": conv_w, "conv_b": conv_b,
            "w_oa": w_oa, "w_ob": w_ob, "w_out": w_out, "ln_g": ln_g, "ln_b": ln_b}


def reference(x, w_in, b_gate, ln_v_g, ln_v_b, w_s, b_s, conv_w, conv_b,
              w_oa, w_ob, w_out, ln_g, ln_b):
    splits = _split_points()
    for l in range(DEPTH):
        p = jnp.einsum('bsd,dc->bsc', x, w_in[l])
        ua, va, za, xb, cb, bb, zb, ga, gb = jnp.split(p, splits, axis=-1)
        ya = chunked_sgu(ua, va, ln_v_g[l], ln_v_b[l], w_s[l], b_s[l]) * jax.nn.silu(za)
        yb = short_gated_conv(xb, cb, bb, conv_w[l], conv_b[l]) * jax.nn.silu(zb)
        gate_a = jax.nn.sigmoid(ga + b_gate[l, :D_MODEL])
        gate_b = jax.nn.sigmoid(gb + b_gate[l, D_MODEL:])
        merged = (gate_a * jnp.einsum('bse,ed->bsd', ya, w_oa[l])
                  + gate_b * jnp.einsum('bse,ed->bsd', yb, w_ob[l]))
        out = jnp.einsum('bsd,de->bse', merged, w_out[l])
        x = layer_norm(DN_ALPHA * x + out, ln_g[l], ln_b[l])
    return x
```

```python
import numpy as np
import concourse.bass as bass
import concourse.mybir as mybir
from concourse.bass_utils import run_bass_kernel_spmd

F32 = mybir.dt.float32
BF16 = mybir.dt.bfloat16
U8 = mybir.dt.uint8
ALU = mybir.AluOpType
AF = mybir.ActivationFunctionType

N_CORES = 8
D = 2048
KC = D // 128
SEQ = 8192
TOK = 2048
PASS = 1024
NPASS = TOK // PASS
NCH = PASS // 128
NTT = PASS // 512
HALO = 2
XOFF = 32
XTW = XOFF + PASS
IN_COLS = 9 * D
LN_EPS = 1e-5
DN_ALPHA = 2.0 ** 0.25

SB_BLK = 64


class _Op:
    __slots__ = ("idx", "eng", "fn", "deps", "dma", "sig", "semkey", "value", "prev_value", "inc")

    def __init__(self, idx, eng, fn, dma):
        self.idx = idx
        self.eng = eng
        self.fn = fn
        self.deps = {}
        self.dma = dma
        self.sig = dma
        self.semkey = None
        self.value = 0
        self.prev_value = 0
        self.inc = 16 if dma else 1


class Sched:
    COMPUTE = ("pe", "act", "dve", "pool")

    def __init__(self, sb_bytes, n_dma_sems=12):
        self.ops = []
        self.nsb = (sb_bytes + SB_BLK - 1) // SB_BLK
        self.lw = {"sb": np.full(self.nsb, -1, np.int64), "ps": np.full(8, -1, np.int64)}
        self.lr = {"sb": {}, "ps": {}}
        self.n_dma_sems = n_dma_sems
        self.dma_count = {"pool": 0, "sp": 0, "act": 0}
        self.dma_sem_uses = {}

    @staticmethod
    def footprint(ap):
        space = str(ap.space) if hasattr(ap, "space") else ""
        tname = type(ap.tensor).__name__
        if "PSum" in tname:
            sp = "ps"
        elif "SBTensor" in tname:
            sp = "sb"
        else:
            return None
        dsz = mybir.dt.size(ap.dtype)
        dims = [tuple(x) for x in list(ap.ap)[1:]]
        off = int(ap.offset)
        if not dims:
            starts = np.array([off], np.int64)
            run = 1
        else:
            *outer, (ls, lc) = dims
            starts = np.array([off], np.int64)
            for (s, c) in outer:
                starts = (starts[:, None] + np.arange(c, dtype=np.int64)[None, :] * s).ravel()
            run = (lc - 1) * abs(ls) + 1
        b0 = starts * dsz
        b1 = (starts + run) * dsz - 1
        if sp == "ps":
            blk0 = b0 // 2048
            blk1 = b1 // 2048
        else:
            blk0 = b0 // SB_BLK
            blk1 = b1 // SB_BLK
        n = int((blk1 - blk0).max()) + 1
        blks = (blk0[:, None] + np.arange(n)[None, :])
        blks = np.minimum(blks, blk1[:, None]).ravel()
        return sp, np.unique(blks)

    def add(self, eng, fn, reads=(), writes=(), dma=False):
        idx = len(self.ops)
        op = _Op(idx, eng, fn, dma)
        if dma:
            n = self.dma_count[eng]
            self.dma_count[eng] = n + 1
            slot = n % self.n_dma_sems
            op.semkey = "dma_%s_%d" % (eng, slot)
            uses = self.dma_sem_uses.get(op.semkey, 0)
            op.prev_value = 16 * uses
            op.value = 16 * (uses + 1)
            self.dma_sem_uses[op.semkey] = uses + 1
            rkey = op.semkey
        else:
            op.semkey = eng
            rkey = eng
        rfp = [f for f in (self.footprint(a) for a in reads) if f is not None]
        wfp = [f for f in (self.footprint(a) for a in writes) if f is not None]
        deps = op.deps
        for sp, blks in rfp:
            for w in np.unique(self.lw[sp][blks]):
                if w >= 0:
                    deps[int(w)] = True
            if sp == "ps":
                for key, arr in self.lr[sp].items():
                    if key != rkey:
                        for r in np.unique(arr[blks]):
                            if r >= 0:
                                deps.setdefault(int(r), False)
        for sp, blks in wfp:
            for w in np.unique(self.lw[sp][blks]):
                if w >= 0:
                    deps.setdefault(int(w), False)
            for key, arr in self.lr[sp].items():
                for r in np.unique(arr[blks]):
                    if r >= 0:
                        deps.setdefault(int(r), False)
        for sp, blks in rfp:
            arr = self.lr[sp].get(rkey)
            if arr is None:
                arr = np.full(len(self.lw[sp]), -1, np.int64)
                self.lr[sp][rkey] = arr
            arr[blks] = idx
        for sp, blks in wfp:
            self.lw[sp][blks] = idx
            for arr in self.lr[sp].values():
                arr[blks] = -1
        deps.pop(idx, None)
        self.ops.append(op)
        return op

    def _needs_wait(self, x, y, raw):
        if y.dma:
            return True
        if y.eng == x.eng and not x.dma:
            if x.eng == "pe":
                return False
            return raw
        return True

    def resolve(self):
        ops = self.ops

        def ltime(y):
            return y.value if y.dma else y.idx + 1

        known = {e: {} for e in ("pe", "act", "dve", "pool", "sp")}
        snaps = [None] * len(ops)
        plan = {e: [] for e in known}
        for x in ops:
            kn = known[x.eng]
            prods = []
            for yi, raw in x.deps.items():
                y = ops[yi]
                if self._needs_wait(x, y, raw):
                    prods.append(y)
            waits = []
            for y in sorted(prods, key=lambda y: -y.idx):
                if kn.get(y.semkey, 0) >= ltime(y):
                    continue
                waits.append(y)
                y.sig = True
                kn[y.semkey] = ltime(y)
                sn = snaps[y.idx]
                if sn:
                    for k, v in sn.items():
                        if kn.get(k, 0) < v:
                            kn[k] = v
            extra = None
            if x.dma and x.prev_value > 0 and kn.get(x.semkey, 0) < x.prev_value:
                extra = (x.semkey, x.prev_value)
                kn[x.semkey] = x.prev_value
            sn = dict(kn)
            sn[x.semkey] = max(sn.get(x.semkey, 0), ltime(x))
            snaps[x.idx] = sn
            plan[x.eng].append((waits, extra, x))
        counters = {e: 0 for e in self.COMPUTE}
        for x in ops:
            if not x.dma and x.sig:
                counters[x.eng] += 1
                x.value = counters[x.eng]
        out = {}
        for e, lst in plan.items():
            o = []
            for waits, extra, x in lst:
                w = {}
                for y in waits:
                    if w.get(y.semkey, 0) < y.value:
                        w[y.semkey] = y.value
                if extra is not None and w.get(extra[0], 0) < extra[1]:
                    w[extra[0]] = extra[1]
                o.append((list(w.items()), x))
            out[e] = o
        self.plan = out
        return out


def build_program():
    nc = bass.Bass("TRN2", target_bir_lowering=False)

    xT_d = nc.dram_tensor("xT", [D, HALO + TOK], F32, kind="ExternalInput").ap()
    xtok_d = nc.dram_tensor("xtok", [TOK, D], F32, kind="ExternalInput").ap()
    w_in_d = nc.dram_tensor("w_in", [D, IN_COLS], F32, kind="ExternalInput").ap()
    w_oa_d = nc.dram_tensor("w_oa", [D, D], F32, kind="ExternalInput").ap()
    w_ob_d = nc.dram_tensor("w_ob", [D, D], F32, kind="ExternalInput").ap()
    w_out_d = nc.dram_tensor("w_out", [D, D], F32, kind="ExternalInput").ap()
    wsT_d = nc.dram_tensor("wsT", [128, 8 * 128], F32, kind="ExternalInput").ap()
    bs_t = nc.dram_tensor("bs", [8, 128], F32, kind="ExternalInput")
    pp_d = nc.dram_tensor("pp", [128, 128], F32, kind="ExternalInput").ap()
    lng_t = nc.dram_tensor("lng", [1, D], F32, kind="ExternalInput")
    lnb_t = nc.dram_tensor("lnb", [1, D], F32, kind="ExternalInput")
    out_d = nc.dram_tensor("out", [TOK, D], F32, kind="ExternalOutput").ap()

    XT_O = 0
    XT_B = KC * XTW * 2
    YAB_O = XT_O + XT_B
    YA_B = KC * PASS * 2
    MG_O = YAB_O + 2 * YA_B
    WR_O = MG_O + YA_B
    NSLAB = 4
    SLAB_B = KC * 256 * 2
    TR_O = WR_O + NSLAB * SLAB_B
    TR_B = 30720
    CN_O = TR_O + TR_B
    C_O = CN_O
    WSB_O = C_O + 8192
    PP_O = WSB_O + 2048
    ST_O = PP_O + 512
    ARENA = ST_O + 2048
    assert ARENA <= 212000, ARENA

    arena = nc.alloc_sbuf_tensor("arena", [128, ARENA], U8)
    ps = nc.alloc_psum_tensor("ps", [128, 8, 512], F32)

    def view(off, nbytes, dt, pattern=None, **kw):
        v = arena[:, off:off + nbytes].bitcast(dt)
        if pattern is not None:
            v = v.rearrange(pattern, **kw)
        return v

    XT = view(XT_O, XT_B, BF16, "p (k t) -> p k t", k=KC)
    YA = view(YAB_O, YA_B, BF16, "p (k t) -> p k t", k=KC)
    YB = view(YAB_O + YA_B, YA_B, BF16, "p (k t) -> p k t", k=KC)
    WOUT = view(YAB_O, 2 * YA_B, BF16, "p (k n) -> p k n", k=KC)
    MG = view(MG_O, YA_B, BF16, "p (k t) -> p k t", k=KC)
    SLABS = [view(WR_O + i * SLAB_B, SLAB_B, BF16, "p (k c) -> p k c", k=KC) for i in range(NSLAB)]
    CC = view(C_O, 8192, F32, "p (k t) -> p k t", k=16)
    WSB = view(WSB_O, 2048, BF16, "p (h t) -> p h t", h=8)
    PP = view(PP_O, 512, F32)
    STAT = view(ST_O, 2048, F32)
    HSAVE = STAT[:, 448:480].rearrange("p (g t) -> p g t", g=16)

    def pp_lnvg(dc): return PP[:, dc:dc + 1]
    def pp_lnvb(dc): return PP[:, 16 + dc:16 + dc + 1]
    def pp_convw(k, g): return PP[:, 32 + k * 16 + g:32 + k * 16 + g + 1]
    def pp_convb(g): return PP[:, 80 + g:80 + g + 1]
    def pp_bgate(i): return PP[:, 96 + i:96 + i + 1]

    WA = view(XT_O, 32768, BF16, "p (k n) -> p k n", k=KC)
    WB = view(YAB_O, 32768, BF16, "p (k n) -> p k n", k=KC)
    YH = [view(YAB_O + YA_B + i * 4096, 4096, F32) for i in range(NCH)]
    Y2 = [view(WR_O + 2 * SLAB_B + i * 4096, 4096, F32) for i in range(4)]

    A_GV = [view(TR_O + i * 1024, 1024, F32) for i in range(8)]
    A_VH = [view(TR_O + 8192 + i * 512, 512, BF16) for i in range(8)]
    A_T = [view(TR_O + 12288 + i * 2048, 2048, F32) for i in range(4)]
    A_SZ = [view(TR_O + 20480 + i * 2048, 2048, F32) for i in range(2)]
    A_MX = [view(TR_O + 24576 + i * 2048, 2048, F32) for i in range(2)]
    B_XB = [view(TR_O + i * 2112, 2112, F32) for i in range(4)]
    B_CV = [view(TR_O + 8448 + i * 2048, 2048, F32) for i in range(4)]
    B_SZ = [view(TR_O + 16640 + i * 2048, 2048, F32) for i in range(2)]
    C_GA = [view(TR_O + i * 2048, 2048, F32) for i in range(4)]
    C_GB = [view(TR_O + 8192 + i * 2048, 2048, F32) for i in range(4)]
    C_GA2 = [view(TR_O + 16384 + i * 2048, 2048, F32) for i in range(4)]
    LNG = view(TR_O + 14336, 8192, F32)
    LNB = view(TR_O + 22528, 8192, F32)
    S_WSF = view(TR_O + 20480, 4096, F32, "p (h t) -> p h t", h=8)
    S_WSF2 = view(TR_O + 20480, 4096, F32)
    S_BS = view(TR_O + 24576, 4096, F32, "p (h t) -> p h t", h=8)
    S_ONE = view(TR_O + 28672, 512, F32)

    S = Sched(ARENA)

    def PE(fn, reads, writes): return S.add("pe", fn, reads, writes)
    def ACT(fn, reads, writes): return S.add("act", fn, reads, writes)
    def DVE(fn, reads, writes): return S.add("dve", fn, reads, writes)
    def POOL(fn, reads, writes): return S.add("pool", fn, reads, writes)
    def DMA(q, out, in_):
        return S.add(q, lambda e, o=out, i=in_: e.dma_start(out=o, in_=i), [in_], [out], dma=True)

    def mm_group(out, pairs):
        n = len(pairs)
        for i, (l, r) in enumerate(pairs):
            PE(lambda e, o=out, l=l, r=r, i=i, n=n: e.matmul(o, l, r, start=(i == 0), stop=(i == n - 1)),
               [l, r], [out])

    def act(out, in_, func, bias=None, scale=None):
        kw = {}
        rd = [in_]
        if bias is not None:
            kw["bias"] = bias
            rd.append(bias)
        if scale is not None:
            kw["scale"] = scale
            if not isinstance(scale, float):
                rd.append(scale)
        ACT(lambda e, o=out, i=in_, f=func, kw=kw: e.activation(out=o, in_=i, func=f, **kw), rd, [out])

    def tt(out, a, b, op, eng=None):
        (eng or DVE)(lambda e, o=out, a=a, b=b, op=op: e.tensor_tensor(out=o, in0=a, in1=b, op=op), [a, b], [out])

    def ts(out, a, s1, s2, op0, op1=None):
        rd = [a] + [s for s in (s1, s2) if s is not None and not isinstance(s, float)]
        if op1 is None:
            DVE(lambda e, o=out, a=a, s1=s1, op0=op0: e.tensor_scalar(out=o, in0=a, scalar1=s1, scalar2=None, op0=op0),
                rd, [out])
        else:
            DVE(lambda e, o=out, a=a, s1=s1, s2=s2, op0=op0, op1=op1:
                e.tensor_scalar(out=o, in0=a, scalar1=s1, scalar2=s2, op0=op0, op1=op1), rd, [out])

    def stt(out, a, s, b, op0, op1):
        rd = [a, b] + ([] if isinstance(s, float) else [s])
        DVE(lambda e, o=out, a=a, s=s, b=b, op0=op0, op1=op1:
            e.scalar_tensor_tensor(out=o, in0=a, scalar=s, in1=b, op0=op0, op1=op1), rd, [out])

    slab_reqs = []
    for _p in range(NPASS):
        for h in range(8):
            slab_reqs += [(w_in_d, 1 * D + h * 256), (w_in_d, 0 * D + h * 256), (w_in_d, 2 * D + h * 256)]
        for jj in range(8):
            slab_reqs += [(w_in_d, (3 + b_) * D + jj * 256) for b_ in range(4)]
        for ee in range(8):
            if ee != 7:
                slab_reqs.append((w_in_d, 7 * D + ee * 256))
            slab_reqs.append((w_in_d, 8 * D + ee * 256))
            if ee == 6:
                slab_reqs.append((w_in_d, 7 * D + 7 * 256))
            slab_reqs += [(w_oa_d, ee * 256), (w_ob_d, ee * 256)]
    SLABS_PER_PASS = len(slab_reqs) // NPASS
    issued = [0]
    limit = [SLABS_PER_PASS]
    taken = [0]

    def issue_upto(i):
        while issued[0] <= min(i, limit[0], len(slab_reqs) - 1):
            j = issued[0]
            src2d, col0 = slab_reqs[j]
            DMA("pool", SLABS[j % NSLAB], src2d[:, col0:col0 + 256].rearrange("(k p) c -> p k c", p=128))
            issued[0] += 1

    def take():
        i = taken[0]
        taken[0] += 1
        assert i <= limit[0]
        issue_upto(i)
        issue_upto(i + NSLAB - 1)
        return SLABS[i % NSLAB]

    def load_xT_cols(p, c0, c1):
        lo = XOFF - HALO if c0 == 0 else XOFF + c0
        slo = p * PASS + (0 if c0 == 0 else HALO + c0)
        DMA("pool", XT[:, :, lo:XOFF + c1],
            xT_d[:, slo:p * PASS + HALO + c1].rearrange("(k p) t -> p k t", p=128))

    def load_xT(p, quarters):
        if p > 0 and list(quarters) == [0, 1, 2, 3]:
            load_xT_cols(p, 0, 512)
            load_xT_cols(p, 512, 1024)
            return
        for q in quarters:
            if p == 0 and q == 0:
                load_xT_cols(p, 0, 128)
                load_xT_cols(p, 128, 256)
            else:
                load_xT_cols(p, q * 256, (q + 1) * 256)

    issue_upto(0)
    load_xT(0, [0, 1, 2, 3])
    issue_upto(1)

    DMA("sp", PP, pp_d)
    DMA("sp", S_WSF, wsT_d.rearrange("p (h t) -> p h t", h=8))
    DMA("sp", S_BS, bass.AP(bs_t, 0, [[0, 128], [128, 8], [1, 128]]))
    NHALF = STAT[:, 128:136]
    DVE(lambda e: e.memset(NHALF, -0.5), [], [NHALF])
    EPS_AP = STAT[:, 144:145]
    DVE(lambda e: e.memset(EPS_AP, LN_EPS), [], [EPS_AP])

    def setup_consts():
        POOL(lambda e: e.affine_select(out=S_WSF, in_=S_WSF, pattern=[[0, 8], [1, 128]],
                                       compare_op=ALU.is_ge, fill=0.0, base=0, channel_multiplier=-1),
             [S_WSF], [S_WSF])
        DVE(lambda e: e.memset(S_ONE, 1.0), [], [S_ONE])
        DVE(lambda e: e.tensor_copy(out=WSB, in_=S_WSF), [S_WSF], [WSB])
        for half in range(2):
            o = ps[:, 2 + half, :]
            r = S_WSF2[:, half * 512:(half + 1) * 512]
            PE(lambda e, o=o, r=r: e.matmul(o, S_ONE, r, start=True, stop=True), [S_ONE, r], [o])
        for dc in range(16):
            h = dc // 2
            rw = ps[:, 2 + h // 4, (h % 4) * 128:(h % 4 + 1) * 128]
            stt(CC[:, dc, :], rw, pp_lnvb(dc), S_BS[:, h, :], ALU.mult, ALU.add)

    def xt_tok(k, t0, n):
        return XT[:, k, XOFF + t0:XOFF + t0 + n]

    for p in range(NPASS):
        for h in range(8):
            wv = take()
            par = h % 2
            MV = STAT[:, 32 + par * 16:32 + par * 16 + 16].rearrange("p (c t) -> p c t", c=8)
            RS = STAT[:, 64 + par * 16:64 + par * 16 + 8]
            for c in range(NCH):
                pv = ps[:, c, 0:256]
                mm_group(pv, [(xt_tok(k, c * 128, 128), wv[:, k, :]) for k in range(KC)])
                gv = A_GV[c]
                act(gv, pv, AF.Gelu_apprx_tanh)
                st6 = STAT[:, (c % 2) * 16:(c % 2) * 16 + 6]
                DVE(lambda e, o=st6, i=gv: e.bn_stats(out=o, in_=i), [gv], [st6])
                DVE(lambda e, o=MV[:, c, :], i=st6: e.bn_aggr(out=o, in_=i), [st6], [MV[:, c, :]])
            VE = STAT[:, 96 + par * 16:96 + par * 16 + 8]
            ts(VE, MV[:, :, 1], LN_EPS, None, ALU.add)
            POOL(lambda e, o=RS, i=VE: e.tensor_tensor(out=o, in0=i, in1=NHALF[:, 0:8], op=ALU.pow), [VE, NHALF[:, 0:8]], [RS])
            for c in range(NCH):
                ts(A_VH[c], A_GV[c], MV[:, c, 0:1], RS[:, c:c + 1], ALU.subtract, ALU.mult)
            wu = take()
            for dh in range(2):
                for t in range(NTT):
                    i = dh * 2 + t
                    pu = ps[:, 2 + (i % 2), :]
                    mm_group(pu, [(wu[:, k, dh * 128:(dh + 1) * 128], xt_tok(k, t * 512, 512)) for k in range(KC)])
                    act(A_T[i], pu, AF.Gelu_apprx_tanh)
            if p == 0 and h == 0:
                setup_consts()
            wz = take()
            for dh in range(2):
                for t in range(NTT):
                    i = dh * 2 + t
                    pz = ps[:, 2 + (i % 2), :]
                    mm_group(pz, [(wz[:, k, dh * 128:(dh + 1) * 128], xt_tok(k, t * 512, 512)) for k in range(KC)])
                    act(A_SZ[i % 2], pz, AF.Silu)
                    tt(A_T[i], A_T[i], A_SZ[i % 2], ALU.mult)
            for dh in range(2):
                for t in range(NTT):
                    i = dh * 2 + t
                    dc = h * 2 + dh
                    pm = ps[:, 4 + i, :]
                    for c4 in range(4):
                        o = pm[:, c4 * 128:(c4 + 1) * 128]
                        l = A_VH[t * 4 + c4][:, dh * 128:(dh + 1) * 128]
                        r = WSB[:, h, :]
                        PE(lambda e, o=o, l=l, r=r: e.matmul(o, l, r, start=True, stop=True), [l, r], [o])
                    mx = A_MX[i % 2]
                    stt(mx.rearrange("p (c t) -> p c t", c=4), pm.rearrange("p (c t) -> p c t", c=4),
                        pp_lnvg(dc), CC[:, dc, :].unsqueeze(1).broadcast_to([128, 4, 128]), ALU.mult, ALU.add)
                    tt(YA[:, dc, t * 512:(t + 1) * 512], mx, A_T[i], ALU.mult)

        for jj in range(8):
            ring = [0]

            def nxt():
                b = 1 + ring[0] % 7
                ring[0] += 1
                return ps[:, b, :]
            wxb = take()
            for j2 in range(2):
                ws = wxb[:, :, j2 * 128:(j2 + 1) * 128]
                if p == 0:
                    ph = ps[:, 0, j2 * 2:j2 * 2 + 2]
                    mm_group(ph, [(ws[:, k, :], XT[:, k, XOFF - HALO:XOFF]) for k in range(KC)])
                    act(B_XB[j2 * 2][:, 0:2], ph, AF.Copy)
                else:
                    DVE(lambda e, o=B_XB[j2 * 2][:, 0:2], i=HSAVE[:, jj * 2 + j2, :]: e.tensor_copy(out=o, in_=i),
                        [HSAVE[:, jj * 2 + j2, :]], [B_XB[j2 * 2][:, 0:2]])
                for t in range(NTT):
                    pt = nxt()
                    mm_group(pt, [(ws[:, k, :], xt_tok(k, t * 512, 512)) for k in range(KC)])
                    act(B_XB[j2 * 2 + t][:, 2:514], pt, AF.Copy)
            wcb = take()
            for j2 in range(2):
                g = jj * 2 + j2
                ws = wcb[:, :, j2 * 128:(j2 + 1) * 128]
                if p == 0:
                    ph = ps[:, 0, 4 + j2 * 2:4 + j2 * 2 + 2]
                    mm_group(ph, [(ws[:, k, :], XT[:, k, XOFF - HALO:XOFF]) for k in range(KC)])
                    tt(B_XB[j2 * 2][:, 0:2], B_XB[j2 * 2][:, 0:2], ph, ALU.mult)
                for t in range(NTT):
                    pt = nxt()
                    hx = B_XB[j2 * 2 + t]
                    cv = B_CV[j2 * 2 + t]
                    mm_group(pt, [(ws[:, k, :], xt_tok(k, t * 512, 512)) for k in range(KC)])
                    tt(hx[:, 2:514], hx[:, 2:514], pt, ALU.mult)
                    if t == 1:
                        DVE(lambda e, o=hx[:, 0:2], i=B_XB[j2 * 2][:, 512:514]: e.tensor_copy(out=o, in_=i),
                            [B_XB[j2 * 2][:, 512:514]], [hx[:, 0:2]])
                        if p + 1 < NPASS:
                            DVE(lambda e, o=HSAVE[:, g, :], i=hx[:, 512:514]: e.tensor_copy(out=o, in_=i),
                                [hx[:, 512:514]], [HSAVE[:, g, :]])
                    act(cv, hx[:, 2:514], AF.Identity, bias=pp_convb(g), scale=pp_convw(2, g))
                    stt(cv, hx[:, 1:513], pp_convw(1, g), cv, ALU.mult, ALU.add)
                    stt(cv, hx[:, 0:512], pp_convw(0, g), cv, ALU.mult, ALU.add)
            wbb = take()
            for j2 in range(2):
                ws = wbb[:, :, j2 * 128:(j2 + 1) * 128]
                for t in range(NTT):
                    pt = nxt()
                    cv = B_CV[j2 * 2 + t]
                    mm_group(pt, [(ws[:, k, :], xt_tok(k, t * 512, 512)) for k in range(KC)])
                    tt(cv, cv, pt, ALU.mult)
            wzb = take()
            for j2 in range(2):
                g = jj * 2 + j2
                ws = wzb[:, :, j2 * 128:(j2 + 1) * 128]
                for t in range(NTT):
                    pt = nxt()
                    i = j2 * 2 + t
                    mm_group(pt, [(ws[:, k, :], xt_tok(k, t * 512, 512)) for k in range(KC)])
                    act(B_SZ[i % 2], pt, AF.Silu)
                    tt(YB[:, g, t * 512:(t + 1) * 512], B_CV[i], B_SZ[i % 2], ALU.mult)

        cring = [0]

        def c_nxt():
            b = cring[0] % 8
            cring[0] += 1
            return ps[:, b, :]

        def c_gate(ee, boff, G):
            wsl = take()
            for e2 in range(2):
                e_ = ee * 2 + e2
                for t in range(NTT):
                    pt = c_nxt()
                    mm_group(pt, [(wsl[:, k, e2 * 128:(e2 + 1) * 128], xt_tok(k, t * 512, 512)) for k in range(KC)])
                    act(G[e2 * 2 + t], pt, AF.Sigmoid, bias=pp_bgate(boff + e_))

        def c_proj(ee, G, Y, GA, final):
            wsl = take()
            for e2 in range(2):
                e_ = ee * 2 + e2
                for t in range(NTT):
                    pt = c_nxt()
                    mm_group(pt, [(wsl[:, k, e2 * 128:(e2 + 1) * 128], Y[:, k, t * 512:(t + 1) * 512]) for k in range(KC)])
                    tt(G[e2 * 2 + t], G[e2 * 2 + t], pt, ALU.mult)
                    if final:
                        tt(MG[:, e_, t * 512:(t + 1) * 512], GA[e2 * 2 + t], C_GB[e2 * 2 + t], ALU.add)

        for ee in range(8):
            GAe = C_GA2 if ee == 7 else C_GA
            if ee != 7:
                c_gate(ee, 0, C_GA)
            c_gate(ee, 16, C_GB)
            if ee == 6:
                c_gate(7, 0, C_GA2)
            c_proj(ee, GAe, YA, GAe, False)
            c_proj(ee, C_GB, YB, GAe, True)

        for n in range(2):
            DMA("pool", WA[:, :, n * 512:(n + 1) * 512],
                w_out_d[:, n * 512:(n + 1) * 512].rearrange("(k p) c -> p k c", p=128))

        LASTP = (p == NPASS - 1)

        def d_load_wb():
            for n in range(2):
                DMA("pool", WB[:, :, n * 512:(n + 1) * 512],
                    w_out_d[:, (2 + n) * 512:(3 + n) * 512].rearrange("(k p) c -> p k c", p=128))


        def d_load_yh(c):
            r0 = p * PASS + c * 128
            DMA("sp", YH[c], xtok_d[r0:r0 + 128, 0:1024])

        d_load_yh(0)
        d_load_yh(1)

        def d_stats(c, n):
            return STAT[:, 192 + c * 24 + n * 6:192 + c * 24 + (n + 1) * 6]

        dcnt = [0]

        def d_evac(c, n, dst, W):
            j = n % 2
            po = ps[:, dcnt[0] % 8, :]
            dcnt[0] += 1
            mm_group(po, [(MG[:, k, c * 128:(c + 1) * 128], W[:, k, j * 512:(j + 1) * 512]) for k in range(KC)])
            blk = dst[:, j * 512:(j + 1) * 512]
            stt(blk, blk, float(DN_ALPHA), po, ALU.mult, ALU.add)
            st = d_stats(c, n)
            DVE(lambda e, o=st, i=blk: e.bn_stats(out=o, in_=i), [blk], [st])

        def d_gb(c):
            y2 = Y2[c % 4]
            beng = DVE if (LASTP and c == NCH - 1) else POOL
            tt(YH[c], YH[c], LNG[:, 0:1024], ALU.mult)
            tt(YH[c], YH[c], LNB[:, 0:1024], ALU.add, eng=beng)
            tt(y2, y2, LNG[:, 1024:2048], ALU.mult)
            tt(y2, y2, LNB[:, 1024:2048], ALU.add, eng=beng)

        def d_out(c, trailing=False):
            r0 = p * PASS + c * 128
            q = "sp" if (trailing and (not LASTP or c == NCH - 1)) else "act"
            DMA(q, out_d[r0:r0 + 128, 0:1024], YH[c])
            DMA(q, out_d[r0:r0 + 128, 1024:2048], Y2[c % 4])

        def sweep1(c):
            for n in range(2):
                d_evac(c, n, YH[c], WA)
            if c + 2 < NCH:
                d_load_yh(c + 2)
            if c == 0:
                d_load_wb()
                limit[0] = (p + 1) * SLABS_PER_PASS + 1
                issue_upto((p + 1) * SLABS_PER_PASS + 1)
            if c == 2:
                DMA("sp", LNG, bass.AP(lng_t, 0, [[0, 128], [1, D]]))
                DMA("sp", LNB, bass.AP(lnb_t, 0, [[0, 128], [1, D]]))

        def sweep2(c):
            r0 = p * PASS + c * 128
            y2 = Y2[c % 4]
            DMA("sp", y2, xtok_d[r0:r0 + 128, 1024:2048])
            last = (LASTP and c == NCH - 1)
            if c >= 1 and not last:
                d_gb(c - 1)
            for n in range(2, 4):
                d_evac(c, n, y2, WB)
            so = 384 + (c % 2) * 32
            mv = STAT[:, so:so + 2]
            rstd = STAT[:, so + 4:so + 5]
            nmr = STAT[:, so + 5:so + 6]
            sd = STAT[:, so + 16:so + 17]
            DVE(lambda e, o=mv, i=STAT[:, 192 + c * 24:192 + (c + 1) * 24]: e.bn_aggr(out=o, in_=i),
                [STAT[:, 192 + c * 24:192 + (c + 1) * 24]], [mv])
            if (not LASTP) and c == NCH - 1:
                ve = STAT[:, so + 8:so + 9]
                ts(ve, STAT[:, so + 1:so + 2], LN_EPS, None, ALU.add)
                POOL(lambda e, o=rstd, i=ve: e.tensor_tensor(out=o, in0=i, in1=NHALF[:, 0:1], op=ALU.pow),
                     [ve, NHALF[:, 0:1]], [rstd])
                ts(YH[c], YH[c], STAT[:, so:so + 1], rstd, ALU.subtract, ALU.mult)
                ts(y2, y2, STAT[:, so:so + 1], rstd, ALU.subtract, ALU.mult)
            else:
                act(sd, STAT[:, so + 1:so + 2], AF.Sqrt, bias=EPS_AP)
                DVE(lambda e, o=rstd, i=sd: e.reciprocal(out=o, in_=i), [sd], [rstd])
                stt(nmr, STAT[:, so:so + 1], -1.0, rstd, ALU.mult, ALU.mult)
                act(YH[c], YH[c], AF.Identity, bias=nmr, scale=rstd)
                act(y2, y2, AF.Identity, bias=nmr, scale=rstd)
            if last:
                d_gb(c - 1)
            if c >= 2:
                d_out(c - 2)

        for c in range(NCH):
            sweep1(c)
        if not LASTP:
            load_xT(p + 1, [0, 1, 2, 3])
        for c in range(NCH):
            sweep2(c)
        d_gb(NCH - 1)
        d_out(NCH - 2, trailing=True)
        d_out(NCH - 1, trailing=True)
        limit[0] = (p + 2) * SLABS_PER_PASS

    plan = S.resolve()

    semkeys = set()
    for x in S.ops:
        if x.sig:
            semkeys.add(x.semkey)
    semkeys = sorted(semkeys)
    sems = {k: nc.alloc_semaphore("s_" + k) for k in semkeys}

    def emit(engine, key):
        for waits, x in plan[key]:
            for k, v in waits:
                engine.wait_ge(sems[k], v)
            ins = x.fn(engine)
            if x.sig:
                ins.then_inc(sems[x.semkey], x.inc)
        if key == "sp":
            for k in semkeys:
                if k.startswith("dma_"):
                    engine.wait_ge(sems[k], 16 * S.dma_sem_uses[k])

    with nc.Block() as block:
        @block.tensor
        def _(e):
            emit(e, "pe")

        @block.scalar
        def _(e):
            emit(e, "act")

        @block.vector
        def _(e):
            emit(e, "dve")

        @block.gpsimd
        def _(e):
            emit(e, "pool")

        @block.sync
        def _(e):
            emit(e, "sp")

    return nc


_NC_CACHE = {}


def _get_program():
    if "nc" not in _NC_CACHE:
        _NC_CACHE["nc"] = build_program()
    return _NC_CACHE["nc"]


def _prepare(x, w_in, b_gate, ln_v_g, ln_v_b, w_s, b_s, conv_w, conv_b, w_oa, w_ob, w_out, ln_g, ln_b,
             cores=range(N_CORES)):
    x = np.asarray(x, dtype=np.float32)
    f = lambda a: np.ascontiguousarray(np.asarray(a, dtype=np.float32))
    w_in0 = f(w_in[0])
    w_oa0 = f(w_oa[0])
    w_ob0 = f(w_ob[0])
    w_out0 = f(w_out[0])
    wsT = f(np.transpose(np.asarray(w_s[0], np.float32), (2, 0, 1)).reshape(128, 8 * 128))
    bs = f(b_s[0])
    pp = np.zeros((128, 128), np.float32)
    pp[:, 0:16] = np.asarray(ln_v_g[0], np.float32).reshape(16, 128).T
    pp[:, 16:32] = np.asarray(ln_v_b[0], np.float32).reshape(16, 128).T
    cw = np.asarray(conv_w[0], np.float32)
    for k in range(3):
        pp[:, 32 + k * 16:48 + k * 16] = cw[k].reshape(16, 128).T
    pp[:, 80:96] = np.asarray(conv_b[0], np.float32).reshape(16, 128).T
    pp[:, 96:128] = np.asarray(b_gate[0], np.float32).reshape(32, 128).T
    lng = f(np.asarray(ln_g[0], np.float32).reshape(1, D))
    lnb = f(np.asarray(ln_b[0], np.float32).reshape(1, D))

    in_maps = []
    for c in cores:
        b = c // (SEQ // TOK)
        s0 = (c % (SEQ // TOK)) * TOK
        xs = x[b, s0:s0 + TOK, :]
        xT = np.zeros((D, HALO + TOK), np.float32)
        xT[:, HALO:] = xs.T
        if s0 > 0:
            xT[:, :HALO] = x[b, s0 - HALO:s0, :].T
        in_maps.append({
            "xT": xT, "xtok": np.ascontiguousarray(xs), "w_in": w_in0, "w_oa": w_oa0, "w_ob": w_ob0,
            "w_out": w_out0, "wsT": wsT, "bs": bs, "pp": pp, "lng": lng, "lnb": lnb,
        })

    return in_maps


def kernel(x, w_in, b_gate, ln_v_g, ln_v_b, w_s, b_s, conv_w, conv_b, w_oa, w_ob, w_out, ln_g, ln_b):
    in_maps = _prepare(x, w_in, b_gate, ln_v_g, ln_v_b, w_s, b_s, conv_w, conv_b, w_oa, w_ob, w_out, ln_g, ln_b)
    nc = _get_program()
    res = run_bass_kernel_spmd(nc, in_maps, core_ids=list(range(N_CORES)))
    out = np.empty((2, SEQ, D), np.float32)
    for c in range(N_CORES):
        b = c // (SEQ // TOK)
        s0 = (c % (SEQ // TOK)) * TOK
        out[b, s0:s0 + TOK, :] = res.results[c]["out"]
    return out
```

```python
import numpy as np
import concourse.bass as bass
import concourse.mybir as mybir
from concourse.bass_utils import run_bass_kernel_spmd

F32 = mybir.dt.float32
BF16 = mybir.dt.bfloat16
U8 = mybir.dt.uint8
ALU = mybir.AluOpType
AF = mybir.ActivationFunctionType

N_CORES = 8
D = 2048
KC = D // 128
SEQ = 8192
TOK = 2048
PASS = 1024
NPASS = TOK // PASS
NCH = PASS // 128
NTT = PASS // 512
HALO = 2
XOFF = 32
XTW = XOFF + PASS
IN_COLS = 9 * D
LN_EPS = 1e-5
DN_ALPHA = 2.0 ** 0.25

SB_BLK = 64


class _Op:
    __slots__ = ("idx", "eng", "fn", "deps", "dma", "sig", "semkey", "value", "prev_value", "inc")

    def __init__(self, idx, eng, fn, dma):
        self.idx = idx
        self.eng = eng
        self.fn = fn
        self.deps = {}
        self.dma = dma
        self.sig = dma
        self.semkey = None
        self.value = 0
        self.prev_value = 0
        self.inc = 16 if dma else 1


class Sched:
    COMPUTE = ("pe", "act", "dve", "pool")

    def __init__(self, sb_bytes, n_dma_sems=12):
        self.ops = []
        self.nsb = (sb_bytes + SB_BLK - 1) // SB_BLK
        self.lw = {"sb": np.full(self.nsb, -1, np.int64), "ps": np.full(8, -1, np.int64)}
        self.lr = {"sb": {}, "ps": {}}
        self.n_dma_sems = n_dma_sems
        self.dma_count = {"pool": 0, "sp": 0, "act": 0}
        self.dma_sem_uses = {}

    @staticmethod
    def footprint(ap):
        space = str(ap.space) if hasattr(ap, "space") else ""
        tname = type(ap.tensor).__name__
        if "PSum" in tname:
            sp = "ps"
        elif "SBTensor" in tname:
            sp = "sb"
        else:
            return None
        dsz = mybir.dt.size(ap.dtype)
        dims = [tuple(x) for x in list(ap.ap)[1:]]
        off = int(ap.offset)
        if not dims:
            starts = np.array([off], np.int64)
            run = 1
        else:
            *outer, (ls, lc) = dims
            starts = np.array([off], np.int64)
            for (s, c) in outer:
                starts = (starts[:, None] + np.arange(c, dtype=np.int64)[None, :] * s).ravel()
            run = (lc - 1) * abs(ls) + 1
        b0 = starts * dsz
        b1 = (starts + run) * dsz - 1
        if sp == "ps":
            blk0 = b0 // 2048
            blk1 = b1 // 2048
        else:
            blk0 = b0 // SB_BLK
            blk1 = b1 // SB_BLK
        n = int((blk1 - blk0).max()) + 1
        blks = (blk0[:, None] + np.arange(n)[None, :])
        blks = np.minimum(blks, blk1[:, None]).ravel()
        return sp, np.unique(blks)

    def add(self, eng, fn, reads=(), writes=(), dma=False):
        idx = len(self.ops)
        op = _Op(idx, eng, fn, dma)
        if dma:
            n = self.dma_count[eng]
            self.dma_count[eng] = n + 1
            slot = n % self.n_dma_sems
            op.semkey = "dma_%s_%d" % (eng, slot)
            uses = self.dma_sem_uses.get(op.semkey, 0)
            op.prev_value = 16 * uses
            op.value = 16 * (uses + 1)
            self.dma_sem_uses[op.semkey] = uses + 1
            rkey = op.semkey
        else:
            op.semkey = eng
            rkey = eng
        rfp = [f for f in (self.footprint(a) for a in reads) if f is not None]
        wfp = [f for f in (self.footprint(a) for a in writes) if f is not None]
        deps = op.deps
        for sp, blks in rfp:
            for w in np.unique(self.lw[sp][blks]):
                if w >= 0:
                    deps[int(w)] = True
            if sp == "ps":
                for key, arr in self.lr[sp].items():
                    if key != rkey:
                        for r in np.unique(arr[blks]):
                            if r >= 0:
                                deps.setdefault(int(r), False)
        for sp, blks in wfp:
            for w in np.unique(self.lw[sp][blks]):
                if w >= 0:
                    deps.setdefault(int(w), False)
            for key, arr in self.lr[sp].items():
                for r in np.unique(arr[blks]):
                    if r >= 0:
                        deps.setdefault(int(r), False)
        for sp, blks in rfp:
            arr = self.lr[sp].get(rkey)
            if arr is None:
                arr = np.full(len(self.lw[sp]), -1, np.int64)
                self.lr[sp][rkey] = arr
            arr[blks] = idx
        for sp, blks in wfp:
            self.lw[sp][blks] = idx
            for arr in self.lr[sp].values():
                arr[blks] = -1
        deps.pop(idx, None)
        self.ops.append(op)
        return op

    def _needs_wait(self, x, y, raw):
        if y.dma:
            return True
        if y.eng == x.eng and not x.dma:
            if x.eng == "pe":
                return False
            return raw
        return True

    def resolve(self):
        ops = self.ops

        def ltime(y):
            return y.value if y.dma else y.idx + 1

        known = {e: {} for e in ("pe", "act", "dve", "pool", "sp")}
        snaps = [None] * len(ops)
        plan = {e: [] for e in known}
        for x in ops:
            kn = known[x.eng]
            prods = []
            for yi, raw in x.deps.items():
                y = ops[yi]
                if self._needs_wait(x, y, raw):
                    prods.append(y)
            waits = []
            for y in sorted(prods, key=lambda y: -y.idx):
                if kn.get(y.semkey, 0) >= ltime(y):
                    continue
                waits.append(y)
                y.sig = True
                kn[y.semkey] = ltime(y)
                sn = snaps[y.idx]
                if sn:
                    for k, v in sn.items():
                        if kn.get(k, 0) < v:
                            kn[k] = v
            extra = None
            if x.dma and x.prev_value > 0 and kn.get(x.semkey, 0) < x.prev_value:
                extra = (x.semkey, x.prev_value)
                kn[x.semkey] = x.prev_value
            sn = dict(kn)
            sn[x.semkey] = max(sn.get(x.semkey, 0), ltime(x))
            snaps[x.idx] = sn
            plan[x.eng].append((waits, extra, x))
        counters = {e: 0 for e in self.COMPUTE}
        for x in ops:
            if not x.dma and x.sig:
                counters[x.eng] += 1
                x.value = counters[x.eng]
        out = {}
        for e, lst in plan.items():
            o = []
            for waits, extra, x in lst:
                w = {}
                for y in waits:
                    if w.get(y.semkey, 0) < y.value:
                        w[y.semkey] = y.value
                if extra is not None and w.get(extra[0], 0) < extra[1]:
                    w[extra[0]] = extra[1]
                o.append((list(w.items()), x))
            out[e] = o
        self.plan = out
        return out


def build_program():
    nc = bass.Bass("TRN2", target_bir_lowering=False)

    xT_d = nc.dram_tensor("xT", [D, HALO + TOK], F32, kind="ExternalInput").ap()
    xtok_d = nc.dram_tensor("xtok", [TOK, D], F32, kind="ExternalInput").ap()
    w_in_d = nc.dram_tensor("w_in", [D, IN_COLS], F32, kind="ExternalInput").ap()
    w_oa_d = nc.dram_tensor("w_oa", [D, D], F32, kind="ExternalInput").ap()
    w_ob_d = nc.dram_tensor("w_ob", [D, D], F32, kind="ExternalInput").ap()
    w_out_d = nc.dram_tensor("w_out", [D, D], F32, kind="ExternalInput").ap()
    wsT_d = nc.dram_tensor("wsT", [128, 8 * 128], F32, kind="ExternalInput").ap()
    bs_t = nc.dram_tensor("bs", [8, 128], F32, kind="ExternalInput")
    pp_d = nc.dram_tensor("pp", [128, 128], F32, kind="ExternalInput").ap()
    lng_t = nc.dram_tensor("lng", [1, D], F32, kind="ExternalInput")
    lnb_t = nc.dram_tensor("lnb", [1, D], F32, kind="ExternalInput")
    out_d = nc.dram_tensor("out", [TOK, D], F32, kind="ExternalOutput").ap()

    XT_O = 0
    XT_B = KC * XTW * 2
    YAB_O = XT_O + XT_B
    YA_B = KC * PASS * 2
    MG_O = YAB_O + 2 * YA_B
    WR_O = MG_O + YA_B
    NSLAB = 4
    SLAB_B = KC * 256 * 2
    TR_O = WR_O + NSLAB * SLAB_B
    TR_B = 30720
    CN_O = TR_O + TR_B
    C_O = CN_O
    WSB_O = C_O + 8192
    PP_O = WSB_O + 2048
    ST_O = PP_O + 512
    ARENA = ST_O + 2048
    assert ARENA <= 212000, ARENA

    arena = nc.alloc_sbuf_tensor("arena", [128, ARENA], U8)
    ps = nc.alloc_psum_tensor("ps", [128, 8, 512], F32)

    def view(off, nbytes, dt, pattern=None, **kw):
        v = arena[:, off:off + nbytes].bitcast(dt)
        if pattern is not None:
            v = v.rearrange(pattern, **kw)
        return v

    XT = view(XT_O, XT_B, BF16, "p (k t) -> p k t", k=KC)
    YA = view(YAB_O, YA_B, BF16, "p (k t) -> p k t", k=KC)
    YB = view(YAB_O + YA_B, YA_B, BF16, "p (k t) -> p k t", k=KC)
    WOUT = view(YAB_O, 2 * YA_B, BF16, "p (k n) -> p k n", k=KC)
    MG = view(MG_O, YA_B, BF16, "p (k t) -> p k t", k=KC)
    SLABS = [view(WR_O + i * SLAB_B, SLAB_B, BF16, "p (k c) -> p k c", k=KC) for i in range(NSLAB)]
    CC = view(C_O, 8192, F32, "p (k t) -> p k t", k=16)
    WSB = view(WSB_O, 2048, BF16, "p (h t) -> p h t", h=8)
    PP = view(PP_O, 512, F32)
    STAT = view(ST_O, 2048, F32)
    HSAVE = STAT[:, 448:480].rearrange("p (g t) -> p g t", g=16)

    def pp_lnvg(dc): return PP[:, dc:dc + 1]
    def pp_lnvb(dc): return PP[:, 16 + dc:16 + dc + 1]
    def pp_convw(k, g): return PP[:, 32 + k * 16 + g:32 + k * 16 + g + 1]
    def pp_convb(g): return PP[:, 80 + g:80 + g + 1]
    def pp_bgate(i): return PP[:, 96 + i:96 + i + 1]

    WA = view(XT_O, 32768, BF16, "p (k n) -> p k n", k=KC)
    WB = view(YAB_O, 32768, BF16, "p (k n) -> p k n", k=KC)
    YH = [view(YAB_O + YA_B + i * 4096, 4096, F32) for i in range(NCH)]
    Y2 = [view(WR_O + 2 * SLAB_B + i * 4096, 4096, F32) for i in range(4)]

    A_GV = [view(TR_O + i * 1024, 1024, F32) for i in range(8)]
    A_VH = [view(TR_O + 8192 + i * 512, 512, BF16) for i in range(8)]
    A_T = [view(TR_O + 12288 + i * 2048, 2048, F32) for i in range(4)]
    A_SZ = [view(TR_O + 20480 + i * 2048, 2048, F32) for i in range(2)]
    A_MX = [view(TR_O + 24576 + i * 2048, 2048, F32) for i in range(2)]
    B_XB = [view(TR_O + i * 2112, 2112, F32) for i in range(4)]
    B_CV = [view(TR_O + 8448 + i * 2048, 2048, F32) for i in range(4)]
    B_SZ = [view(TR_O + 16640 + i * 2048, 2048, F32) for i in range(2)]
    C_GA = [view(TR_O + i * 2048, 2048, F32) for i in range(4)]
    C_GB = [view(TR_O + 8192 + i * 2048, 2048, F32) for i in range(4)]
    C_GA2 = [view(TR_O + 16384 + i * 2048, 2048, F32) for i in range(4)]
    LNG = view(TR_O + 14336, 8192, F32)
    LNB = view(TR_O + 22528, 8192, F32)
    S_WSF = view(TR_O + 20480, 4096, F32, "p (h t) -> p h t", h=8)
    S_WSF2 = view(TR_O + 20480, 4096, F32)
    S_BS = view(TR_O + 24576, 4096, F32, "p (h t) -> p h t", h=8)
    S_ONE = view(TR_O + 28672, 512, F32)

    S = Sched(ARENA)

    def PE(fn, reads, writes): return S.add("pe", fn, reads, writes)
    def ACT(fn, reads, writes): return S.add("act", fn, reads, writes)
    def DVE(fn, reads, writes): return S.add("dve", fn, reads, writes)
    def POOL(fn, reads, writes): return S.add("pool", fn, reads, writes)
    def DMA(q, out, in_):
        return S.add(q, lambda e, o=out, i=in_: e.dma_start(out=o, in_=i), [in_], [out], dma=True)

    def mm_group(out, pairs):
        n = len(pairs)
        for i, (l, r) in enumerate(pairs):
            PE(lambda e, o=out, l=l, r=r, i=i, n=n: e.matmul(o, l, r, start=(i == 0), stop=(i == n - 1)),
               [l, r], [out])

    def act(out, in_, func, bias=None, scale=None):
        kw = {}
        rd = [in_]
        if bias is not None:
            kw["bias"] = bias
            rd.append(bias)
        if scale is not None:
            kw["scale"] = scale
            if not isinstance(scale, float):
                rd.append(scale)
        ACT(lambda e, o=out, i=in_, f=func, kw=kw: e.activation(out=o, in_=i, func=f, **kw), rd, [out])

    def tt(out, a, b, op, eng=None):
        (eng or DVE)(lambda e, o=out, a=a, b=b, op=op: e.tensor_tensor(out=o, in0=a, in1=b, op=op), [a, b], [out])

    def ts(out, a, s1, s2, op0, op1=None):
        rd = [a] + [s for s in (s1, s2) if s is not None and not isinstance(s, float)]
        if op1 is None:
            DVE(lambda e, o=out, a=a, s1=s1, op0=op0: e.tensor_scalar(out=o, in0=a, scalar1=s1, scalar2=None, op0=op0),
                rd, [out])
        else:
            DVE(lambda e, o=out, a=a, s1=s1, s2=s2, op0=op0, op1=op1:
                e.tensor_scalar(out=o, in0=a, scalar1=s1, scalar2=s2, op0=op0, op1=op1), rd, [out])

    def stt(out, a, s, b, op0, op1):
        rd = [a, b] + ([] if isinstance(s, float) else [s])
        DVE(lambda e, o=out, a=a, s=s, b=b, op0=op0, op1=op1:
            e.scalar_tensor_tensor(out=o, in0=a, scalar=s, in1=b, op0=op0, op1=op1), rd, [out])

    slab_reqs = []
    for _p in range(NPASS):
        for h in range(8):
            slab_reqs += [(w_in_d, 1 * D + h * 256), (w_in_d, 0 * D + h * 256), (w_in_d, 2 * D + h * 256)]
        for jj in range(8):
            slab_reqs += [(w_in_d, (3 + b_) * D + jj * 256) for b_ in range(4)]
        for ee in range(8):
            if ee != 7:
                slab_reqs.append((w_in_d, 7 * D + ee * 256))
            slab_reqs.append((w_in_d, 8 * D + ee * 256))
            if ee == 6:
                slab_reqs.append((w_in_d, 7 * D + 7 * 256))
            slab_reqs += [(w_oa_d, ee * 256), (w_ob_d, ee * 256)]
    SLABS_PER_PASS = len(slab_reqs) // NPASS
    issued = [0]
    limit = [SLABS_PER_PASS]
    taken = [0]

    def issue_upto(i):
        while issued[0] <= min(i, limit[0], len(slab_reqs) - 1):
            j = issued[0]
            src2d, col0 = slab_reqs[j]
            DMA("pool", SLABS[j % NSLAB], src2d[:, col0:col0 + 256].rearrange("(k p) c -> p k c", p=128))
            issued[0] += 1

    def take():
        i = taken[0]
        taken[0] += 1
        assert i <= limit[0]
        issue_upto(i)
        issue_upto(i + NSLAB - 1)
        return SLABS[i % NSLAB]

    def load_xT_cols(p, c0, c1):
        lo = XOFF - HALO if c0 == 0 else XOFF + c0
        slo = p * PASS + (0 if c0 == 0 else HALO + c0)
        DMA("pool", XT[:, :, lo:XOFF + c1],
            xT_d[:, slo:p * PASS + HALO + c1].rearrange("(k p) t -> p k t", p=128))

    def load_xT(p, quarters):
        if p > 0 and list(quarters) == [0, 1, 2, 3]:
            load_xT_cols(p, 0, 512)
            load_xT_cols(p, 512, 1024)
            return
        for q in quarters:
            if p == 0 and q == 0:
                load_xT_cols(p, 0, 128)
                load_xT_cols(p, 128, 256)
            else:
                load_xT_cols(p, q * 256, (q + 1) * 256)

    issue_upto(0)
    load_xT(0, [0, 1, 2, 3])
    issue_upto(1)

    DMA("sp", PP, pp_d)
    DMA("sp", S_WSF, wsT_d.rearrange("p (h t) -> p h t", h=8))
    DMA("sp", S_BS, bass.AP(bs_t, 0, [[0, 128], [128, 8], [1, 128]]))
    NHALF = STAT[:, 128:136]
    DVE(lambda e: e.memset(NHALF, -0.5), [], [NHALF])
    EPS_AP = STAT[:, 144:145]
    DVE(lambda e: e.memset(EPS_AP, LN_EPS), [], [EPS_AP])

    def setup_consts():
        POOL(lambda e: e.affine_select(out=S_WSF, in_=S_WSF, pattern=[[0, 8], [1, 128]],
                                       compare_op=ALU.is_ge, fill=0.0, base=0, channel_multiplier=-1),
             [S_WSF], [S_WSF])
        DVE(lambda e: e.memset(S_ONE, 1.0), [], [S_ONE])
        DVE(lambda e: e.tensor_copy(out=WSB, in_=S_WSF), [S_WSF], [WSB])
        for half in range(2):
            o = ps[:, 2 + half, :]
            r = S_WSF2[:, half * 512:(half + 1) * 512]
            PE(lambda e, o=o, r=r: e.matmul(o, S_ONE, r, start=True, stop=True), [S_ONE, r], [o])
        for dc in range(16):
            h = dc // 2
            rw = ps[:, 2 + h // 4, (h % 4) * 128:(h % 4 + 1) * 128]
            stt(CC[:, dc, :], rw, pp_lnvb(dc), S_BS[:, h, :], ALU.mult, ALU.add)

    def xt_tok(k, t0, n):
        return XT[:, k, XOFF + t0:XOFF + t0 + n]

    for p in range(NPASS):
        for h in range(8):
            wv = take()
            par = h % 2
            MV = STAT[:, 32 + par * 16:32 + par * 16 + 16].rearrange("p (c t) -> p c t", c=8)
            RS = STAT[:, 64 + par * 16:64 + par * 16 + 8]
            for c in range(NCH):
                pv = ps[:, c, 0:256]
                mm_group(pv, [(xt_tok(k, c * 128, 128), wv[:, k, :]) for k in range(KC)])
                gv = A_GV[c]
                act(gv, pv, AF.Gelu_apprx_tanh)
                st6 = STAT[:, (c % 2) * 16:(c % 2) * 16 + 6]
                DVE(lambda e, o=st6, i=gv: e.bn_stats(out=o, in_=i), [gv], [st6])
                DVE(lambda e, o=MV[:, c, :], i=st6: e.bn_aggr(out=o, in_=i), [st6], [MV[:, c, :]])
            VE = STAT[:, 96 + par * 16:96 + par * 16 + 8]
            ts(VE, MV[:, :, 1], LN_EPS, None, ALU.add)
            POOL(lambda e, o=RS, i=VE: e.tensor_tensor(out=o, in0=i, in1=NHALF[:, 0:8], op=ALU.pow), [VE, NHALF[:, 0:8]], [RS])
            for c in range(NCH):
                ts(A_VH[c], A_GV[c], MV[:, c, 0:1], RS[:, c:c + 1], ALU.subtract, ALU.mult)
            wu = take()
            for dh in range(2):
                for t in range(NTT):
                    i = dh * 2 + t
                    pu = ps[:, i % 4, :]
                    mm_group(pu, [(wu[:, k, dh * 128:(dh + 1) * 128], xt_tok(k, t * 512, 512)) for k in range(KC)])
                    act(A_T[i], pu, AF.Gelu_apprx_tanh)
            if p == 0 and h == 0:
                setup_consts()
            wz = take()
            for dh in range(2):
                for t in range(NTT):
                    i = dh * 2 + t
                    pz = ps[:, i % 4, :]
                    mm_group(pz, [(wz[:, k, dh * 128:(dh + 1) * 128], xt_tok(k, t * 512, 512)) for k in range(KC)])
                    act(A_SZ[i % 2], pz, AF.Silu)
                    tt(A_T[i], A_T[i], A_SZ[i % 2], ALU.mult)
            for dh in range(2):
                for t in range(NTT):
                    i = dh * 2 + t
                    dc = h * 2 + dh
                    pm = ps[:, 4 + i, :]
                    for c4 in range(4):
                        o = pm[:, c4 * 128:(c4 + 1) * 128]
                        l = A_VH[t * 4 + c4][:, dh * 128:(dh + 1) * 128]
                        r = WSB[:, h, :]
                        PE(lambda e, o=o, l=l, r=r: e.matmul(o, l, r, start=True, stop=True), [l, r], [o])
                    mx = A_MX[i % 2]
                    stt(mx.rearrange("p (c t) -> p c t", c=4), pm.rearrange("p (c t) -> p c t", c=4),
                        pp_lnvg(dc), CC[:, dc, :].unsqueeze(1).broadcast_to([128, 4, 128]), ALU.mult, ALU.add)
                    tt(YA[:, dc, t * 512:(t + 1) * 512], mx, A_T[i], ALU.mult)

        for jj in range(8):
            ring = [0]

            def nxt():
                b = 1 + ring[0] % 7
                ring[0] += 1
                return ps[:, b, :]
            wxb = take()
            for j2 in range(2):
                ws = wxb[:, :, j2 * 128:(j2 + 1) * 128]
                if p == 0:
                    ph = ps[:, 0, j2 * 2:j2 * 2 + 2]
                    mm_group(ph, [(ws[:, k, :], XT[:, k, XOFF - HALO:XOFF]) for k in range(KC)])
                    act(B_XB[j2 * 2][:, 0:2], ph, AF.Copy)
                else:
                    DVE(lambda e, o=B_XB[j2 * 2][:, 0:2], i=HSAVE[:, jj * 2 + j2, :]: e.tensor_copy(out=o, in_=i),
                        [HSAVE[:, jj * 2 + j2, :]], [B_XB[j2 * 2][:, 0:2]])
                for t in range(NTT):
                    pt = nxt()
                    mm_group(pt, [(ws[:, k, :], xt_tok(k, t * 512, 512)) for k in range(KC)])
                    act(B_XB[j2 * 2 + t][:, 2:514], pt, AF.Copy)
            wcb = take()
            for j2 in range(2):
                g = jj * 2 + j2
                ws = wcb[:, :, j2 * 128:(j2 + 1) * 128]
                if p == 0:
                    ph = ps[:, 0, 4 + j2 * 2:4 + j2 * 2 + 2]
                    mm_group(ph, [(ws[:, k, :], XT[:, k, XOFF - HALO:XOFF]) for k in range(KC)])
                    tt(B_XB[j2 * 2][:, 0:2], B_XB[j2 * 2][:, 0:2], ph, ALU.mult)
                for t in range(NTT):
                    pt = nxt()
                    hx = B_XB[j2 * 2 + t]
                    cv = B_CV[j2 * 2 + t]
                    mm_group(pt, [(ws[:, k, :], xt_tok(k, t * 512, 512)) for k in range(KC)])
                    tt(hx[:, 2:514], hx[:, 2:514], pt, ALU.mult)
                    if t == 1:
                        DVE(lambda e, o=hx[:, 0:2], i=B_XB[j2 * 2][:, 512:514]: e.tensor_copy(out=o, in_=i),
                            [B_XB[j2 * 2][:, 512:514]], [hx[:, 0:2]])
                        if p + 1 < NPASS:
                            DVE(lambda e, o=HSAVE[:, g, :], i=hx[:, 512:514]: e.tensor_copy(out=o, in_=i),
                                [hx[:, 512:514]], [HSAVE[:, g, :]])
                    act(cv, hx[:, 2:514], AF.Identity, bias=pp_convb(g), scale=pp_convw(2, g))
                    stt(cv, hx[:, 1:513], pp_convw(1, g), cv, ALU.mult, ALU.add)
                    stt(cv, hx[:, 0:512], pp_convw(0, g), cv, ALU.mult, ALU.add)
            wbb = take()
            for j2 in range(2):
                ws = wbb[:, :, j2 * 128:(j2 + 1) * 128]
                for t in range(NTT):
                    pt = nxt()
                    cv = B_CV[j2 * 2 + t]
                    mm_group(pt, [(ws[:, k, :], xt_tok(k, t * 512, 512)) for k in range(KC)])
                    tt(cv, cv, pt, ALU.mult)
            wzb = take()
            for j2 in range(2):
                g = jj * 2 + j2
                ws = wzb[:, :, j2 * 128:(j2 + 1) * 128]
                for t in range(NTT):
                    pt = nxt()
                    i = j2 * 2 + t
                    mm_group(pt, [(ws[:, k, :], xt_tok(k, t * 512, 512)) for k in range(KC)])
                    act(B_SZ[i % 2], pt, AF.Silu)
                    tt(YB[:, g, t * 512:(t + 1) * 512], B_CV[i], B_SZ[i % 2], ALU.mult)

        cring = [0]

        def c_nxt():
            b = cring[0] % 8
            cring[0] += 1
            return ps[:, b, :]

        def c_gate(ee, boff, G):
            wsl = take()
            for e2 in range(2):
                e_ = ee * 2 + e2
                for t in range(NTT):
                    pt = c_nxt()
                    mm_group(pt, [(wsl[:, k, e2 * 128:(e2 + 1) * 128], xt_tok(k, t * 512, 512)) for k in range(KC)])
                    act(G[e2 * 2 + t], pt, AF.Sigmoid, bias=pp_bgate(boff + e_))

        def c_proj(ee, G, Y, GA, final):
            wsl = take()
            for e2 in range(2):
                e_ = ee * 2 + e2
                for t in range(NTT):
                    pt = c_nxt()
                    mm_group(pt, [(wsl[:, k, e2 * 128:(e2 + 1) * 128], Y[:, k, t * 512:(t + 1) * 512]) for k in range(KC)])
                    tt(G[e2 * 2 + t], G[e2 * 2 + t], pt, ALU.mult)
                    if final:
                        tt(MG[:, e_, t * 512:(t + 1) * 512], GA[e2 * 2 + t], C_GB[e2 * 2 + t], ALU.add)

        for ee in range(8):
            GAe = C_GA2 if ee == 7 else C_GA
            if ee != 7:
                c_gate(ee, 0, C_GA)
            c_gate(ee, 16, C_GB)
            if ee == 6:
                c_gate(7, 0, C_GA2)
            c_proj(ee, GAe, YA, GAe, False)
            c_proj(ee, C_GB, YB, GAe, True)

        for n in range(2):
            DMA("pool", WA[:, :, n * 512:(n + 1) * 512],
                w_out_d[:, n * 512:(n + 1) * 512].rearrange("(k p) c -> p k c", p=128))

        LASTP = (p == NPASS - 1)

        def d_load_wb():
            for n in range(2):
                DMA("pool", WB[:, :, n * 512:(n + 1) * 512],
                    w_out_d[:, (2 + n) * 512:(3 + n) * 512].rearrange("(k p) c -> p k c", p=128))


        def d_load_yh(c):
            r0 = p * PASS + c * 128
            DMA("sp", YH[c], xtok_d[r0:r0 + 128, 0:1024])

        d_load_yh(0)
        d_load_yh(1)

        def d_stats(c, n):
            return STAT[:, 192 + c * 24 + n * 6:192 + c * 24 + (n + 1) * 6]

        dcnt = [0]

        def d_evac(c, n, dst, W):
            j = n % 2
            po = ps[:, dcnt[0] % 8, :]
            dcnt[0] += 1
            mm_group(po, [(MG[:, k, c * 128:(c + 1) * 128], W[:, k, j * 512:(j + 1) * 512]) for k in range(KC)])
            blk = dst[:, j * 512:(j + 1) * 512]
            stt(blk, blk, float(DN_ALPHA), po, ALU.mult, ALU.add)
            st = d_stats(c, n)
            DVE(lambda e, o=st, i=blk: e.bn_stats(out=o, in_=i), [blk], [st])

        def d_gb(c):
            y2 = Y2[c % 4]
            beng = DVE if (LASTP and c == NCH - 1) else POOL
            tt(YH[c], YH[c], LNG[:, 0:1024], ALU.mult)
            tt(YH[c], YH[c], LNB[:, 0:1024], ALU.add, eng=beng)
            tt(y2, y2, LNG[:, 1024:2048], ALU.mult)
            tt(y2, y2, LNB[:, 1024:2048], ALU.add, eng=beng)

        def d_out(c, trailing=False):
            r0 = p * PASS + c * 128
            q = "sp" if (trailing and (not LASTP or c == NCH - 1)) else "act"
            DMA(q, out_d[r0:r0 + 128, 0:1024], YH[c])
            DMA(q, out_d[r0:r0 + 128, 1024:2048], Y2[c % 4])

        def sweep1(c):
            for n in range(2):
                d_evac(c, n, YH[c], WA)
            if c + 2 < NCH:
                d_load_yh(c + 2)
            if c == 0:
                d_load_wb()
                limit[0] = (p + 1) * SLABS_PER_PASS + 1
                issue_upto((p + 1) * SLABS_PER_PASS + 1)
            if c == 2:
                DMA("sp", LNG, bass.AP(lng_t, 0, [[0, 128], [1, D]]))
                DMA("sp", LNB, bass.AP(lnb_t, 0, [[0, 128], [1, D]]))

        def sweep2(c):
            r0 = p * PASS + c * 128
            y2 = Y2[c % 4]
            DMA("sp", y2, xtok_d[r0:r0 + 128, 1024:2048])
            last = (LASTP and c == NCH - 1)
            if c >= 1 and not last:
                d_gb(c - 1)
            for n in range(2, 4):
                d_evac(c, n, y2, WB)
            so = 384 + (c % 2) * 32
            mv = STAT[:, so:so + 2]
            rstd = STAT[:, so + 4:so + 5]
            nmr = STAT[:, so + 5:so + 6]
            sd = STAT[:, so + 16:so + 17]
            DVE(lambda e, o=mv, i=STAT[:, 192 + c * 24:192 + (c + 1) * 24]: e.bn_aggr(out=o, in_=i),
                [STAT[:, 192 + c * 24:192 + (c + 1) * 24]], [mv])
            if (not LASTP) and c == NCH - 1:
                ve = STAT[:, so + 8:so + 9]
                ts(ve, STAT[:, so + 1:so + 2], LN_EPS, None, ALU.add)
                POOL(lambda e, o=rstd, i=ve: e.tensor_tensor(out=o, in0=i, in1=NHALF[:, 0:1], op=ALU.pow),
                     [ve, NHALF[:, 0:1]], [rstd])
                ts(YH[c], YH[c], STAT[:, so:so + 1], rstd, ALU.subtract, ALU.mult)
                ts(y2, y2, STAT[:, so:so + 1], rstd, ALU.subtract, ALU.mult)
            else:
                act(sd, STAT[:, so + 1:so + 2], AF.Sqrt, bias=EPS_AP)
                DVE(lambda e, o=rstd, i=sd: e.reciprocal(out=o, in_=i), [sd], [rstd])
                stt(nmr, STAT[:, so:so + 1], -1.0, rstd, ALU.mult, ALU.mult)
                act(YH[c], YH[c], AF.Identity, bias=nmr, scale=rstd)
                act(y2, y2, AF.Identity, bias=nmr, scale=rstd)
            if last:
                d_gb(c - 1)
            if c >= 2:
                d_out(c - 2)

        for c in range(NCH):
            sweep1(c)
        if not LASTP:
            load_xT(p + 1, [0, 1, 2, 3])
        for c in range(NCH):
            sweep2(c)
        d_gb(NCH - 1)
        d_out(NCH - 2, trailing=True)
        d_out(NCH - 1, trailing=True)
        limit[0] = (p + 2) * SLABS_PER_PASS

    plan = S.resolve()

    semkeys = set()
    for x in S.ops:
        if x.sig:
            semkeys.add(x.semkey)
    semkeys = sorted(semkeys)
    sems = {k: nc.alloc_semaphore("s_" + k) for k in semkeys}

    def emit(engine, key):
        for waits, x in plan[key]:
            for k, v in waits:
                engine.wait_ge(sems[k], v)
            ins = x.fn(engine)
            if x.sig:
                ins.then_inc(sems[x.semkey], x.inc)
        if key == "sp":
            for k in semkeys:
                if k.startswith("dma_"):
                    engine.wait_ge(sems[k], 16 * S.dma_sem_uses[k])

    with nc.Block() as block:
        @block.tensor
        def _(e):
            emit(e, "pe")

        @block.scalar
        def _(e):
            emit(e, "act")

        @block.vector
        def _(e):
            emit(e, "dve")

        @block.gpsimd
        def _(e):
            emit(e, "pool")

        @block.sync
        def _(e):
            emit(e, "sp")

    return nc


_NC_CACHE = {}


def _get_program():
    if "nc" not in _NC_CACHE:
        _NC_CACHE["nc"] = build_program()
    return _NC_CACHE["nc"]


def _prepare(x, w_in, b_gate, ln_v_g, ln_v_b, w_s, b_s, conv_w, conv_b, w_oa, w_ob, w_out, ln_g, ln_b,
             cores=range(N_CORES)):
    x = np.asarray(x, dtype=np.float32)
    f = lambda a: np.ascontiguousarray(np.asarray(a, dtype=np.float32))
    w_in0 = f(w_in[0])
    w_oa0 = f(w_oa[0])
    w_ob0 = f(w_ob[0])
    w_out0 = f(w_out[0])
    wsT = f(np.transpose(np.asarray(w_s[0], np.float32), (2, 0, 1)).reshape(128, 8 * 128))
    bs = f(b_s[0])
    pp = np.zeros((128, 128), np.float32)
    pp[:, 0:16] = np.asarray(ln_v_g[0], np.float32).reshape(16, 128).T
    pp[:, 16:32] = np.asarray(ln_v_b[0], np.float32).reshape(16, 128).T
    cw = np.asarray(conv_w[0], np.float32)
    for k in range(3):
        pp[:, 32 + k * 16:48 + k * 16] = cw[k].reshape(16, 128).T
    pp[:, 80:96] = np.asarray(conv_b[0], np.float32).reshape(16, 128).T
    pp[:, 96:128] = np.asarray(b_gate[0], np.float32).reshape(32, 128).T
    lng = f(np.asarray(ln_g[0], np.float32).reshape(1, D))
    lnb = f(np.asarray(ln_b[0], np.float32).reshape(1, D))

    in_maps = []
    for c in cores:
        b = c // (SEQ // TOK)
        s0 = (c % (SEQ // TOK)) * TOK
        xs = x[b, s0:s0 + TOK, :]
        xT = np.zeros((D, HALO + TOK), np.float32)
        xT[:, HALO:] = xs.T
        if s0 > 0:
            xT[:, :HALO] = x[b, s0 - HALO:s0, :].T
        in_maps.append({
            "xT": xT, "xtok": np.ascontiguousarray(xs), "w_in": w_in0, "w_oa": w_oa0, "w_ob": w_ob0,
            "w_out": w_out0, "wsT": wsT, "bs": bs, "pp": pp, "lng": lng, "lnb": lnb,
        })

    return in_maps


def kernel(x, w_in, b_gate, ln_v_g, ln_v_b, w_s, b_s, conv_w, conv_b, w_oa, w_ob, w_out, ln_g, ln_b):
    in_maps = _prepare(x, w_in, b_gate, ln_v_g, ln_v_b, w_s, b_s, conv_w, conv_b, w_oa, w_ob, w_out, ln_g, ln_b)
    nc = _get_program()
    res = run_bass_kernel_spmd(nc, in_maps, core_ids=list(range(N_CORES)))
    out = np.empty((2, SEQ, D), np.float32)
    for c in range(N_CORES):
        b = c // (SEQ // TOK)
        s0 = (c % (SEQ // TOK)) * TOK
        out[b, s0:s0 + TOK, :] = res.results[c]["out"]
    return out
```

```python
import numpy as np
import concourse.bass as bass
import concourse.mybir as mybir
from concourse.bass_utils import run_bass_kernel_spmd

F32 = mybir.dt.float32
BF16 = mybir.dt.bfloat16
U8 = mybir.dt.uint8
ALU = mybir.AluOpType
AF = mybir.ActivationFunctionType

N_CORES = 8
D = 2048
KC = D // 128
SEQ = 8192
TOK = 2048
PASS = 1024
NPASS = TOK // PASS
NCH = PASS // 128
NTT = PASS // 512
HALO = 2
XOFF = 32
XTW = XOFF + PASS
IN_COLS = 9 * D
LN_EPS = 1e-5
DN_ALPHA = 2.0 ** 0.25

SB_BLK = 64


class _Op:
    __slots__ = ("idx", "eng", "fn", "deps", "dma", "sig", "semkey", "value", "prev_value", "inc")

    def __init__(self, idx, eng, fn, dma):
        self.idx = idx
        self.eng = eng
        self.fn = fn
        self.deps = {}
        self.dma = dma
        self.sig = dma
        self.semkey = None
        self.value = 0
        self.prev_value = 0
        self.inc = 16 if dma else 1


class Sched:
    COMPUTE = ("pe", "act", "dve", "pool")

    def __init__(self, sb_bytes, n_dma_sems=12):
        self.ops = []
        self.nsb = (sb_bytes + SB_BLK - 1) // SB_BLK
        self.lw = {"sb": np.full(self.nsb, -1, np.int64), "ps": np.full(8, -1, np.int64)}
        self.lr = {"sb": {}, "ps": {}}
        self.n_dma_sems = n_dma_sems
        self.dma_count = {"pool": 0, "sp": 0, "act": 0}
        self.dma_sem_uses = {}

    @staticmethod
    def footprint(ap):
        space = str(ap.space) if hasattr(ap, "space") else ""
        tname = type(ap.tensor).__name__
        if "PSum" in tname:
            sp = "ps"
        elif "SBTensor" in tname:
            sp = "sb"
        else:
            return None
        dsz = mybir.dt.size(ap.dtype)
        dims = [tuple(x) for x in list(ap.ap)[1:]]
        off = int(ap.offset)
        if not dims:
            starts = np.array([off], np.int64)
            run = 1
        else:
            *outer, (ls, lc) = dims
            starts = np.array([off], np.int64)
            for (s, c) in outer:
                starts = (starts[:, None] + np.arange(c, dtype=np.int64)[None, :] * s).ravel()
            run = (lc - 1) * abs(ls) + 1
        b0 = starts * dsz
        b1 = (starts + run) * dsz - 1
        if sp == "ps":
            blk0 = b0 // 2048
            blk1 = b1 // 2048
        else:
            blk0 = b0 // SB_BLK
            blk1 = b1 // SB_BLK
        n = int((blk1 - blk0).max()) + 1
        blks = (blk0[:, None] + np.arange(n)[None, :])
        blks = np.minimum(blks, blk1[:, None]).ravel()
        return sp, np.unique(blks)

    def add(self, eng, fn, reads=(), writes=(), dma=False):
        idx = len(self.ops)
        op = _Op(idx, eng, fn, dma)
        if dma:
            n = self.dma_count[eng]
            self.dma_count[eng] = n + 1
            slot = n % self.n_dma_sems
            op.semkey = "dma_%s_%d" % (eng, slot)
            uses = self.dma_sem_uses.get(op.semkey, 0)
            op.prev_value = 16 * uses
            op.value = 16 * (uses + 1)
            self.dma_sem_uses[op.semkey] = uses + 1
            rkey = op.semkey
        else:
            op.semkey = eng
            rkey = eng
        rfp = [f for f in (self.footprint(a) for a in reads) if f is not None]
        wfp = [f for f in (self.footprint(a) for a in writes) if f is not None]
        deps = op.deps
        for sp, blks in rfp:
            for w in np.unique(self.lw[sp][blks]):
                if w >= 0:
                    deps[int(w)] = True
            if sp == "ps":
                for key, arr in self.lr[sp].items():
                    if key != rkey:
                        for r in np.unique(arr[blks]):
                            if r >= 0:
                                deps.setdefault(int(r), False)
        for sp, blks in wfp:
            for w in np.unique(self.lw[sp][blks]):
                if w >= 0:
                    deps.setdefault(int(w), False)
            for key, arr in self.lr[sp].items():
                for r in np.unique(arr[blks]):
                    if r >= 0:
                        deps.setdefault(int(r), False)
        for sp, blks in rfp:
            arr = self.lr[sp].get(rkey)
            if arr is None:
                arr = np.full(len(self.lw[sp]), -1, np.int64)
                self.lr[sp][rkey] = arr
            arr[blks] = idx
        for sp, blks in wfp:
            self.lw[sp][blks] = idx
            for arr in self.lr[sp].values():
                arr[blks] = -1
        deps.pop(idx, None)
        self.ops.append(op)
        return op

    def _needs_wait(self, x, y, raw):
        if y.dma:
            return True
        if y.eng == x.eng and not x.dma:
            if x.eng == "pe":
                return False
            return raw
        return True

    def resolve(self):
        ops = self.ops

        def ltime(y):
            return y.value if y.dma else y.idx + 1

        known = {e: {} for e in ("pe", "act", "dve", "pool", "sp")}
        snaps = [None] * len(ops)
        plan = {e: [] for e in known}
        for x in ops:
            kn = known[x.eng]
            prods = []
            for yi, raw in x.deps.items():
                y = ops[yi]
                if self._needs_wait(x, y, raw):
                    prods.append(y)
            waits = []
            for y in sorted(prods, key=lambda y: -y.idx):
                if kn.get(y.semkey, 0) >= ltime(y):
                    continue
                waits.append(y)
                y.sig = True
                kn[y.semkey] = ltime(y)
                sn = snaps[y.idx]
                if sn:
                    for k, v in sn.items():
                        if kn.get(k, 0) < v:
                            kn[k] = v
            extra = None
            if x.dma and x.prev_value > 0 and kn.get(x.semkey, 0) < x.prev_value:
                extra = (x.semkey, x.prev_value)
                kn[x.semkey] = x.prev_value
            sn = dict(kn)
            sn[x.semkey] = max(sn.get(x.semkey, 0), ltime(x))
            snaps[x.idx] = sn
            plan[x.eng].append((waits, extra, x))
        counters = {e: 0 for e in self.COMPUTE}
        for x in ops:
            if not x.dma and x.sig:
                counters[x.eng] += 1
                x.value = counters[x.eng]
        out = {}
        for e, lst in plan.items():
            o = []
            for waits, extra, x in lst:
                w = {}
                for y in waits:
                    if w.get(y.semkey, 0) < y.value:
                        w[y.semkey] = y.value
                if extra is not None and w.get(extra[0], 0) < extra[1]:
                    w[extra[0]] = extra[1]
                o.append((list(w.items()), x))
            out[e] = o
        self.plan = out
        return out


def build_program():
    nc = bass.Bass("TRN2", target_bir_lowering=False)

    xT_d = nc.dram_tensor("xT", [D, HALO + TOK], F32, kind="ExternalInput").ap()
    xtok_d = nc.dram_tensor("xtok", [TOK, D], F32, kind="ExternalInput").ap()
    w_in_d = nc.dram_tensor("w_in", [D, IN_COLS], F32, kind="ExternalInput").ap()
    w_oa_d = nc.dram_tensor("w_oa", [D, D], F32, kind="ExternalInput").ap()
    w_ob_d = nc.dram_tensor("w_ob", [D, D], F32, kind="ExternalInput").ap()
    w_out_d = nc.dram_tensor("w_out", [D, D], F32, kind="ExternalInput").ap()
    wsT_d = nc.dram_tensor("wsT", [128, 8 * 128], F32, kind="ExternalInput").ap()
    bs_t = nc.dram_tensor("bs", [8, 128], F32, kind="ExternalInput")
    pp_d = nc.dram_tensor("pp", [128, 128], F32, kind="ExternalInput").ap()
    lng_t = nc.dram_tensor("lng", [1, D], F32, kind="ExternalInput")
    lnb_t = nc.dram_tensor("lnb", [1, D], F32, kind="ExternalInput")
    out_d = nc.dram_tensor("out", [TOK, D], F32, kind="ExternalOutput").ap()

    XT_O = 0
    XT_B = KC * XTW * 2
    YAB_O = XT_O + XT_B
    YA_B = KC * PASS * 2
    MG_O = YAB_O + 2 * YA_B
    WR_O = MG_O + YA_B
    NSLAB = 4
    SLAB_B = KC * 256 * 2
    TR_O = WR_O + NSLAB * SLAB_B
    TR_B = 30720
    CN_O = TR_O + TR_B
    C_O = CN_O
    WSB_O = C_O + 8192
    PP_O = WSB_O + 2048
    ST_O = PP_O + 512
    ARENA = ST_O + 2048
    assert ARENA <= 212000, ARENA

    arena = nc.alloc_sbuf_tensor("arena", [128, ARENA], U8)
    ps = nc.alloc_psum_tensor("ps", [128, 8, 512], F32)

    def view(off, nbytes, dt, pattern=None, **kw):
        v = arena[:, off:off + nbytes].bitcast(dt)
        if pattern is not None:
            v = v.rearrange(pattern, **kw)
        return v

    XT = view(XT_O, XT_B, BF16, "p (k t) -> p k t", k=KC)
    YA = view(YAB_O, YA_B, BF16, "p (k t) -> p k t", k=KC)
    YB = view(YAB_O + YA_B, YA_B, BF16, "p (k t) -> p k t", k=KC)
    WOUT = view(YAB_O, 2 * YA_B, BF16, "p (k n) -> p k n", k=KC)
    MG = view(MG_O, YA_B, BF16, "p (k t) -> p k t", k=KC)
    SLABS = [view(WR_O + i * SLAB_B, SLAB_B, BF16, "p (k c) -> p k c", k=KC) for i in range(NSLAB)]
    CC = view(C_O, 8192, F32, "p (k t) -> p k t", k=16)
    WSB = view(WSB_O, 2048, BF16, "p (h t) -> p h t", h=8)
    PP = view(PP_O, 512, F32)
    STAT = view(ST_O, 2048, F32)
    HSAVE = STAT[:, 448:480].rearrange("p (g t) -> p g t", g=16)

    def pp_lnvg(dc): return PP[:, dc:dc + 1]
    def pp_lnvb(dc): return PP[:, 16 + dc:16 + dc + 1]
    def pp_convw(k, g): return PP[:, 32 + k * 16 + g:32 + k * 16 + g + 1]
    def pp_convb(g): return PP[:, 80 + g:80 + g + 1]
    def pp_bgate(i): return PP[:, 96 + i:96 + i + 1]

    WA = view(XT_O, 32768, BF16, "p (k n) -> p k n", k=KC)
    WB = view(YAB_O, 32768, BF16, "p (k n) -> p k n", k=KC)
    YH = [view(YAB_O + YA_B + i * 4096, 4096, F32) for i in range(NCH)]
    Y2 = [view(WR_O + 2 * SLAB_B + i * 4096, 4096, F32) for i in range(4)]

    A_GV = [view(TR_O + i * 1024, 1024, F32) for i in range(8)]
    A_VH = [view(TR_O + 8192 + i * 512, 512, BF16) for i in range(8)]
    A_T = [view(TR_O + 12288 + i * 2048, 2048, F32) for i in range(4)]
    A_SZ = [view(TR_O + 20480 + i * 2048, 2048, F32) for i in range(2)]
    A_MX = [view(TR_O + 24576 + i * 2048, 2048, F32) for i in range(2)]
    B_XB = [view(TR_O + i * 2112, 2112, F32) for i in range(4)]
    B_CV = [view(TR_O + 8448 + i * 2048, 2048, F32) for i in range(4)]
    B_SZ = [view(TR_O + 16640 + i * 2048, 2048, F32) for i in range(2)]
    C_GA = [view(TR_O + i * 2048, 2048, F32) for i in range(4)]
    C_GB = [view(TR_O + 8192 + i * 2048, 2048, F32) for i in range(4)]
    C_GA2 = [view(TR_O + 16384 + i * 2048, 2048, F32) for i in range(4)]
    LNG = view(TR_O + 14336, 8192, F32)
    LNB = view(TR_O + 22528, 8192, F32)
    S_WSF = view(TR_O + 20480, 4096, F32, "p (h t) -> p h t", h=8)
    S_WSF2 = view(TR_O + 20480, 4096, F32)
    S_BS = view(TR_O + 24576, 4096, F32, "p (h t) -> p h t", h=8)
    S_ONE = view(TR_O + 28672, 512, F32)

    S = Sched(ARENA)

    def PE(fn, reads, writes): return S.add("pe", fn, reads, writes)
    def ACT(fn, reads, writes): return S.add("act", fn, reads, writes)
    def DVE(fn, reads, writes): return S.add("dve", fn, reads, writes)
    def POOL(fn, reads, writes): return S.add("pool", fn, reads, writes)
    def DMA(q, out, in_):
        return S.add(q, lambda e, o=out, i=in_: e.dma_start(out=o, in_=i), [in_], [out], dma=True)

    def mm_group(out, pairs):
        n = len(pairs)
        for i, (l, r) in enumerate(pairs):
            PE(lambda e, o=out, l=l, r=r, i=i, n=n: e.matmul(o, l, r, start=(i == 0), stop=(i == n - 1)),
               [l, r], [out])

    def act(out, in_, func, bias=None, scale=None):
        kw = {}
        rd = [in_]
        if bias is not None:
            kw["bias"] = bias
            rd.append(bias)
        if scale is not None:
            kw["scale"] = scale
            if not isinstance(scale, float):
                rd.append(scale)
        ACT(lambda e, o=out, i=in_, f=func, kw=kw: e.activation(out=o, in_=i, func=f, **kw), rd, [out])

    def tt(out, a, b, op, eng=None):
        (eng or DVE)(lambda e, o=out, a=a, b=b, op=op: e.tensor_tensor(out=o, in0=a, in1=b, op=op), [a, b], [out])

    def ts(out, a, s1, s2, op0, op1=None):
        rd = [a] + [s for s in (s1, s2) if s is not None and not isinstance(s, float)]
        if op1 is None:
            DVE(lambda e, o=out, a=a, s1=s1, op0=op0: e.tensor_scalar(out=o, in0=a, scalar1=s1, scalar2=None, op0=op0),
                rd, [out])
        else:
            DVE(lambda e, o=out, a=a, s1=s1, s2=s2, op0=op0, op1=op1:
                e.tensor_scalar(out=o, in0=a, scalar1=s1, scalar2=s2, op0=op0, op1=op1), rd, [out])

    def stt(out, a, s, b, op0, op1):
        rd = [a, b] + ([] if isinstance(s, float) else [s])
        DVE(lambda e, o=out, a=a, s=s, b=b, op0=op0, op1=op1:
            e.scalar_tensor_tensor(out=o, in0=a, scalar=s, in1=b, op0=op0, op1=op1), rd, [out])

    slab_reqs = []
    for _p in range(NPASS):
        for h in range(8):
            slab_reqs += [(w_in_d, 1 * D + h * 256), (w_in_d, 0 * D + h * 256), (w_in_d, 2 * D + h * 256)]
        for jj in range(8):
            slab_reqs += [(w_in_d, (3 + b_) * D + jj * 256) for b_ in range(4)]
        for ee in range(8):
            if ee != 7:
                slab_reqs.append((w_in_d, 7 * D + ee * 256))
            slab_reqs.append((w_in_d, 8 * D + ee * 256))
            if ee == 6:
                slab_reqs.append((w_in_d, 7 * D + 7 * 256))
            slab_reqs += [(w_oa_d, ee * 256), (w_ob_d, ee * 256)]
    SLABS_PER_PASS = len(slab_reqs) // NPASS
    issued = [0]
    limit = [SLABS_PER_PASS]
    taken = [0]

    def issue_upto(i):
        while issued[0] <= min(i, limit[0], len(slab_reqs) - 1):
            j = issued[0]
            src2d, col0 = slab_reqs[j]
            DMA("pool", SLABS[j % NSLAB], src2d[:, col0:col0 + 256].rearrange("(k p) c -> p k c", p=128))
            issued[0] += 1

    def take():
        i = taken[0]
        taken[0] += 1
        assert i <= limit[0]
        issue_upto(i)
        issue_upto(i + NSLAB - 1)
        return SLABS[i % NSLAB]

    def load_xT_cols(p, c0, c1):
        lo = XOFF - HALO if c0 == 0 else XOFF + c0
        slo = p * PASS + (0 if c0 == 0 else HALO + c0)
        DMA("pool", XT[:, :, lo:XOFF + c1],
            xT_d[:, slo:p * PASS + HALO + c1].rearrange("(k p) t -> p k t", p=128))

    def load_xT(p, quarters):
        if p > 0 and list(quarters) == [0, 1, 2, 3]:
            load_xT_cols(p, 0, 512)
            load_xT_cols(p, 512, 1024)
            return
        for q in quarters:
            if p == 0 and q == 0:
                load_xT_cols(p, 0, 128)
                load_xT_cols(p, 128, 256)
            else:
                load_xT_cols(p, q * 256, (q + 1) * 256)

    issue_upto(0)
    load_xT(0, [0, 1, 2, 3])
    issue_upto(1)

    DMA("sp", PP, pp_d)
    DMA("sp", S_WSF, wsT_d.rearrange("p (h t) -> p h t", h=8))
    DMA("sp", S_BS, bass.AP(bs_t, 0, [[0, 128], [128, 8], [1, 128]]))
    NHALF = STAT[:, 128:136]
    DVE(lambda e: e.memset(NHALF, -0.5), [], [NHALF])
    EPS_AP = STAT[:, 144:145]
    DVE(lambda e: e.memset(EPS_AP, LN_EPS), [], [EPS_AP])

    def setup_consts():
        POOL(lambda e: e.affine_select(out=S_WSF, in_=S_WSF, pattern=[[0, 8], [1, 128]],
                                       compare_op=ALU.is_ge, fill=0.0, base=0, channel_multiplier=-1),
             [S_WSF], [S_WSF])
        DVE(lambda e: e.memset(S_ONE, 1.0), [], [S_ONE])
        DVE(lambda e: e.tensor_copy(out=WSB, in_=S_WSF), [S_WSF], [WSB])
        for half in range(2):
            o = ps[:, 2 + half, :]
            r = S_WSF2[:, half * 512:(half + 1) * 512]
            PE(lambda e, o=o, r=r: e.matmul(o, S_ONE, r, start=True, stop=True), [S_ONE, r], [o])
        for dc in range(16):
            h = dc // 2
            rw = ps[:, 2 + h // 4, (h % 4) * 128:(h % 4 + 1) * 128]
            stt(CC[:, dc, :], rw, pp_lnvb(dc), S_BS[:, h, :], ALU.mult, ALU.add)

    def xt_tok(k, t0, n):
        return XT[:, k, XOFF + t0:XOFF + t0 + n]

    for p in range(NPASS):
        for h in range(8):
            wv = take()
            par = h % 2
            MV = STAT[:, 32 + par * 16:32 + par * 16 + 16].rearrange("p (c t) -> p c t", c=8)
            RS = STAT[:, 64 + par * 16:64 + par * 16 + 8]
            for c in range(NCH):
                pv = ps[:, c, 0:256]
                mm_group(pv, [(xt_tok(k, c * 128, 128), wv[:, k, :]) for k in range(KC)])
                gv = A_GV[c]
                act(gv, pv, AF.Gelu_apprx_tanh)
                st6 = STAT[:, (c % 2) * 16:(c % 2) * 16 + 6]
                DVE(lambda e, o=st6, i=gv: e.bn_stats(out=o, in_=i), [gv], [st6])
                DVE(lambda e, o=MV[:, c, :], i=st6: e.bn_aggr(out=o, in_=i), [st6], [MV[:, c, :]])
            VE = STAT[:, 96 + par * 16:96 + par * 16 + 8]
            ts(VE, MV[:, :, 1], LN_EPS, None, ALU.add)
            POOL(lambda e, o=RS, i=VE: e.tensor_tensor(out=o, in0=i, in1=NHALF[:, 0:8], op=ALU.pow), [VE, NHALF[:, 0:8]], [RS])
            for c in range(NCH):
                ts(A_VH[c], A_GV[c], MV[:, c, 0:1], RS[:, c:c + 1], ALU.subtract, ALU.mult)
            wu = take()
            for dh in range(2):
                for t in range(NTT):
                    i = dh * 2 + t
                    pu = ps[:, 2 + (i % 2), :]
                    mm_group(pu, [(wu[:, k, dh * 128:(dh + 1) * 128], xt_tok(k, t * 512, 512)) for k in range(KC)])
                    act(A_T[i], pu, AF.Gelu_apprx_tanh)
            if p == 0 and h == 0:
                setup_consts()
            wz = take()
            for dh in range(2):
                for t in range(NTT):
                    i = dh * 2 + t
                    pz = ps[:, 2 + (i % 2), :]
                    mm_group(pz, [(wz[:, k, dh * 128:(dh + 1) * 128], xt_tok(k, t * 512, 512)) for k in range(KC)])
                    act(A_SZ[i % 2], pz, AF.Silu)
                    tt(A_T[i], A_T[i], A_SZ[i % 2], ALU.mult)
            for dh in range(2):
                for t in range(NTT):
                    i = dh * 2 + t
                    dc = h * 2 + dh
                    pm = ps[:, 4 + i, :]
                    for c4 in range(4):
                        o = pm[:, c4 * 128:(c4 + 1) * 128]
                        l = A_VH[t * 4 + c4][:, dh * 128:(dh + 1) * 128]
                        r = WSB[:, h, :]
                        PE(lambda e, o=o, l=l, r=r: e.matmul(o, l, r, start=True, stop=True), [l, r], [o])
                    mx = A_MX[i % 2]
                    stt(mx.rearrange("p (c t) -> p c t", c=4), pm.rearrange("p (c t) -> p c t", c=4),
                        pp_lnvg(dc), CC[:, dc, :].unsqueeze(1).broadcast_to([128, 4, 128]), ALU.mult, ALU.add)
                    tt(YA[:, dc, t * 512:(t + 1) * 512], mx, A_T[i], ALU.mult)

        for jj in range(8):
            ring = [0]

            def nxt():
                b = 1 + ring[0] % 7
                ring[0] += 1
                return ps[:, b, :]
            wxb = take()
            for j2 in range(2):
                ws = wxb[:, :, j2 * 128:(j2 + 1) * 128]
                if p == 0:
                    ph = ps[:, 0, j2 * 2:j2 * 2 + 2]
                    mm_group(ph, [(ws[:, k, :], XT[:, k, XOFF - HALO:XOFF]) for k in range(KC)])
                    act(B_XB[j2 * 2][:, 0:2], ph, AF.Copy)
                else:
                    DVE(lambda e, o=B_XB[j2 * 2][:, 0:2], i=HSAVE[:, jj * 2 + j2, :]: e.tensor_copy(out=o, in_=i),
                        [HSAVE[:, jj * 2 + j2, :]], [B_XB[j2 * 2][:, 0:2]])
                for t in range(NTT):
                    pt = nxt()
                    mm_group(pt, [(ws[:, k, :], xt_tok(k, t * 512, 512)) for k in range(KC)])
                    act(B_XB[j2 * 2 + t][:, 2:514], pt, AF.Copy)
            wcb = take()
            for j2 in range(2):
                g = jj * 2 + j2
                ws = wcb[:, :, j2 * 128:(j2 + 1) * 128]
                if p == 0:
                    ph = ps[:, 0, 4 + j2 * 2:4 + j2 * 2 + 2]
                    mm_group(ph, [(ws[:, k, :], XT[:, k, XOFF - HALO:XOFF]) for k in range(KC)])
                    tt(B_XB[j2 * 2][:, 0:2], B_XB[j2 * 2][:, 0:2], ph, ALU.mult)
                for t in range(NTT):
                    pt = nxt()
                    hx = B_XB[j2 * 2 + t]
                    cv = B_CV[j2 * 2 + t]
                    mm_group(pt, [(ws[:, k, :], xt_tok(k, t * 512, 512)) for k in range(KC)])
                    tt(hx[:, 2:514], hx[:, 2:514], pt, ALU.mult)
                    if t == 1:
                        DVE(lambda e, o=hx[:, 0:2], i=B_XB[j2 * 2][:, 512:514]: e.tensor_copy(out=o, in_=i),
                            [B_XB[j2 * 2][:, 512:514]], [hx[:, 0:2]])
                        if p + 1 < NPASS:
                            DVE(lambda e, o=HSAVE[:, g, :], i=hx[:, 512:514]: e.tensor_copy(out=o, in_=i),
                                [hx[:, 512:514]], [HSAVE[:, g, :]])
                    act(cv, hx[:, 2:514], AF.Identity, bias=pp_convb(g), scale=pp_convw(2, g))
                    stt(cv, hx[:, 1:513], pp_convw(1, g), cv, ALU.mult, ALU.add)
                    stt(cv, hx[:, 0:512], pp_convw(0, g), cv, ALU.mult, ALU.add)
            wbb = take()
            for j2 in range(2):
                ws = wbb[:, :, j2 * 128:(j2 + 1) * 128]
                for t in range(NTT):
                    pt = nxt()
                    cv = B_CV[j2 * 2 + t]
                    mm_group(pt, [(ws[:, k, :], xt_tok(k, t * 512, 512)) for k in range(KC)])
                    tt(cv, cv, pt, ALU.mult)
            wzb = take()
            for j2 in range(2):
                g = jj * 2 + j2
                ws = wzb[:, :, j2 * 128:(j2 + 1) * 128]
                for t in range(NTT):
                    pt = nxt()
                    i = j2 * 2 + t
                    mm_group(pt, [(ws[:, k, :], xt_tok(k, t * 512, 512)) for k in range(KC)])
                    act(B_SZ[i % 2], pt, AF.Silu)
                    tt(YB[:, g, t * 512:(t + 1) * 512], B_CV[i], B_SZ[i % 2], ALU.mult)

        cring = [0]

        def c_nxt():
            b = cring[0] % 8
            cring[0] += 1
            return ps[:, b, :]

        def c_gate(ee, boff, G):
            wsl = take()
            for e2 in range(2):
                e_ = ee * 2 + e2
                for t in range(NTT):
                    pt = c_nxt()
                    mm_group(pt, [(wsl[:, k, e2 * 128:(e2 + 1) * 128], xt_tok(k, t * 512, 512)) for k in range(KC)])
                    act(G[e2 * 2 + t], pt, AF.Sigmoid, bias=pp_bgate(boff + e_))

        def c_proj(ee, G, Y, GA, final):
            wsl = take()
            for e2 in range(2):
                e_ = ee * 2 + e2
                for t in range(NTT):
                    pt = c_nxt()
                    mm_group(pt, [(wsl[:, k, e2 * 128:(e2 + 1) * 128], Y[:, k, t * 512:(t + 1) * 512]) for k in range(KC)])
                    tt(G[e2 * 2 + t], G[e2 * 2 + t], pt, ALU.mult)
                    if final:
                        tt(MG[:, e_, t * 512:(t + 1) * 512], GA[e2 * 2 + t], C_GB[e2 * 2 + t], ALU.add)

        for ee in range(8):
            GAe = C_GA2 if ee == 7 else C_GA
            if ee != 7:
                c_gate(ee, 0, C_GA)
            c_gate(ee, 16, C_GB)
            if ee == 6:
                c_gate(7, 0, C_GA2)
            c_proj(ee, GAe, YA, GAe, False)
            c_proj(ee, C_GB, YB, GAe, True)

        for n in range(2):
            DMA("pool", WA[:, :, n * 512:(n + 1) * 512],
                w_out_d[:, n * 512:(n + 1) * 512].rearrange("(k p) c -> p k c", p=128))

        LASTP = (p == NPASS - 1)

        def d_load_wb():
            for n in range(2):
                DMA("pool", WB[:, :, n * 512:(n + 1) * 512],
                    w_out_d[:, (2 + n) * 512:(3 + n) * 512].rearrange("(k p) c -> p k c", p=128))


        def d_load_yh(c):
            r0 = p * PASS + c * 128
            DMA("sp", YH[c], xtok_d[r0:r0 + 128, 0:1024])

        d_load_yh(0)
        d_load_yh(1)

        def d_stats(c, n):
            return STAT[:, 192 + c * 24 + n * 6:192 + c * 24 + (n + 1) * 6]

        dcnt = [0]

        def d_evac(c, n, dst, W):
            j = n % 2
            po = ps[:, dcnt[0] % 8, :]
            dcnt[0] += 1
            mm_group(po, [(MG[:, k, c * 128:(c + 1) * 128], W[:, k, j * 512:(j + 1) * 512]) for k in range(KC)])
            blk = dst[:, j * 512:(j + 1) * 512]
            stt(blk, blk, float(DN_ALPHA), po, ALU.mult, ALU.add)
            st = d_stats(c, n)
            DVE(lambda e, o=st, i=blk: e.bn_stats(out=o, in_=i), [blk], [st])

        def d_gb(c):
            y2 = Y2[c % 4]
            beng = DVE if (LASTP and c == NCH - 1) else POOL
            tt(YH[c], YH[c], LNG[:, 0:1024], ALU.mult)
            tt(YH[c], YH[c], LNB[:, 0:1024], ALU.add, eng=beng)
            tt(y2, y2, LNG[:, 1024:2048], ALU.mult)
            tt(y2, y2, LNB[:, 1024:2048], ALU.add, eng=beng)

        def d_out(c, trailing=False):
            r0 = p * PASS + c * 128
            q = "sp" if (trailing and (not LASTP or c == NCH - 1)) else "act"
            DMA(q, out_d[r0:r0 + 128, 0:1024], YH[c])
            DMA(q, out_d[r0:r0 + 128, 1024:2048], Y2[c % 4])

        def sweep1(c):
            for n in range(2):
                d_evac(c, n, YH[c], WA)
            if c + 2 < NCH:
                d_load_yh(c + 2)
            if c == 0:
                d_load_wb()
                limit[0] = (p + 1) * SLABS_PER_PASS + 1
                issue_upto((p + 1) * SLABS_PER_PASS + 1)
            if c == 2:
                DMA("sp", LNG, bass.AP(lng_t, 0, [[0, 128], [1, D]]))
                DMA("sp", LNB, bass.AP(lnb_t, 0, [[0, 128], [1, D]]))

        def sweep2(c):
            r0 = p * PASS + c * 128
            y2 = Y2[c % 4]
            DMA("sp", y2, xtok_d[r0:r0 + 128, 1024:2048])
            last = (LASTP and c == NCH - 1)
            if c >= 1 and not last:
                d_gb(c - 1)
            for n in range(2, 4):
                d_evac(c, n, y2, WB)
            so = 384 + (c % 2) * 32
            mv = STAT[:, so:so + 2]
            rstd = STAT[:, so + 4:so + 5]
            nmr = STAT[:, so + 5:so + 6]
            sd = STAT[:, so + 16:so + 17]
            DVE(lambda e, o=mv, i=STAT[:, 192 + c * 24:192 + (c + 1) * 24]: e.bn_aggr(out=o, in_=i),
                [STAT[:, 192 + c * 24:192 + (c + 1) * 24]], [mv])
            if (not LASTP) and c == NCH - 1:
                ve = STAT[:, so + 8:so + 9]
                ts(ve, STAT[:, so + 1:so + 2], LN_EPS, None, ALU.add)
                POOL(lambda e, o=rstd, i=ve: e.tensor_tensor(out=o, in0=i, in1=NHALF[:, 0:1], op=ALU.pow),
                     [ve, NHALF[:, 0:1]], [rstd])
                ts(YH[c], YH[c], STAT[:, so:so + 1], rstd, ALU.subtract, ALU.mult)
                ts(y2, y2, STAT[:, so:so + 1], rstd, ALU.subtract, ALU.mult)
            else:
                act(sd, STAT[:, so + 1:so + 2], AF.Sqrt, bias=EPS_AP)
                DVE(lambda e, o=rstd, i=sd: e.reciprocal(out=o, in_=i), [sd], [rstd])
                stt(nmr, STAT[:, so:so + 1], -1.0, rstd, ALU.mult, ALU.mult)
                act(YH[c], YH[c], AF.Identity, bias=nmr, scale=rstd)
                act(y2, y2, AF.Identity, bias=nmr, scale=rstd)
            if last:
                d_gb(c - 1)
            if c >= 2:
                d_out(c - 2)

        for c in range(NCH):
            sweep1(c)
        if not LASTP:
            load_xT(p + 1, [0, 1, 2, 3])
        for c in range(NCH):
            sweep2(c)
        if not LASTP:
            limit[0] = (p + 1) * SLABS_PER_PASS + 2
            issue_upto((p + 1) * SLABS_PER_PASS + 2)
        d_gb(NCH - 1)
        d_out(NCH - 2, trailing=True)
        d_out(NCH - 1, trailing=True)
        limit[0] = (p + 2) * SLABS_PER_PASS

    plan = S.resolve()

    semkeys = set()
    for x in S.ops:
        if x.sig:
            semkeys.add(x.semkey)
    semkeys = sorted(semkeys)
    sems = {k: nc.alloc_semaphore("s_" + k) for k in semkeys}

    def emit(engine, key):
        for waits, x in plan[key]:
            for k, v in waits:
                engine.wait_ge(sems[k], v)
            ins = x.fn(engine)
            if x.sig:
                ins.then_inc(sems[x.semkey], x.inc)
        if key == "sp":
            for k in semkeys:
                if k.startswith("dma_"):
                    engine.wait_ge(sems[k], 16 * S.dma_sem_uses[k])

    with nc.Block() as block:
        @block.tensor
        def _(e):
            emit(e, "pe")

        @block.scalar
        def _(e):
            emit(e, "act")

        @block.vector
        def _(e):
            emit(e, "dve")

        @block.gpsimd
        def _(e):
            emit(e, "pool")

        @block.sync
        def _(e):
            emit(e, "sp")

    return nc


_NC_CACHE = {}


def _get_program():
    if "nc" not in _NC_CACHE:
        _NC_CACHE["nc"] = build_program()
    return _NC_CACHE["nc"]


def _prepare(x, w_in, b_gate, ln_v_g, ln_v_b, w_s, b_s, conv_w, conv_b, w_oa, w_ob, w_out, ln_g, ln_b,
             cores=range(N_CORES)):
    x = np.asarray(x, dtype=np.float32)
    f = lambda a: np.ascontiguousarray(np.asarray(a, dtype=np.float32))
    w_in0 = f(w_in[0])
    w_oa0 = f(w_oa[0])
    w_ob0 = f(w_ob[0])
    w_out0 = f(w_out[0])
    wsT = f(np.transpose(np.asarray(w_s[0], np.float32), (2, 0, 1)).reshape(128, 8 * 128))
    bs = f(b_s[0])
    pp = np.zeros((128, 128), np.float32)
    pp[:, 0:16] = np.asarray(ln_v_g[0], np.float32).reshape(16, 128).T
    pp[:, 16:32] = np.asarray(ln_v_b[0], np.float32).reshape(16, 128).T
    cw = np.asarray(conv_w[0], np.float32)
    for k in range(3):
        pp[:, 32 + k * 16:48 + k * 16] = cw[k].reshape(16, 128).T
    pp[:, 80:96] = np.asarray(conv_b[0], np.float32).reshape(16, 128).T
    pp[:, 96:128] = np.asarray(b_gate[0], np.float32).reshape(32, 128).T
    lng = f(np.asarray(ln_g[0], np.float32).reshape(1, D))
    lnb = f(np.asarray(ln_b[0], np.float32).reshape(1, D))

    in_maps = []
    for c in cores:
        b = c // (SEQ // TOK)
        s0 = (c % (SEQ // TOK)) * TOK
        xs = x[b, s0:s0 + TOK, :]
        xT = np.zeros((D, HALO + TOK), np.float32)
        xT[:, HALO:] = xs.T
        if s0 > 0:
            xT[:, :HALO] = x[b, s0 - HALO:s0, :].T
        in_maps.append({
            "xT": xT, "xtok": np.ascontiguousarray(xs), "w_in": w_in0, "w_oa": w_oa0, "w_ob": w_ob0,
            "w_out": w_out0, "wsT": wsT, "bs": bs, "pp": pp, "lng": lng, "lnb": lnb,
        })

    return in_maps


def kernel(x, w_in, b_gate, ln_v_g, ln_v_b, w_s, b_s, conv_w, conv_b, w_oa, w_ob, w_out, ln_g, ln_b):
    in_maps = _prepare(x, w_in, b_gate, ln_v_g, ln_v_b, w_s, b_s, conv_w, conv_b, w_oa, w_ob, w_out, ln_g, ln_b)
    nc = _get_program()
    res = run_bass_kernel_spmd(nc, in_maps, core_ids=list(range(N_CORES)))
    out = np.empty((2, SEQ, D), np.float32)
    for c in range(N_CORES):
        b = c // (SEQ // TOK)
        s0 = (c % (SEQ // TOK)) * TOK
        out[b, s0:s0 + TOK, :] = res.results[c]["out"]
    return out
```

```python
import numpy as np
import concourse.bass as bass
import concourse.mybir as mybir
from concourse.bass_utils import run_bass_kernel_spmd

F32 = mybir.dt.float32
BF16 = mybir.dt.bfloat16
U8 = mybir.dt.uint8
ALU = mybir.AluOpType
AF = mybir.ActivationFunctionType

N_CORES = 8
D = 2048
KC = D // 128
SEQ = 8192
TOK = 2048
PASS = 1024
NPASS = TOK // PASS
NCH = PASS // 128
NTT = PASS // 512
HALO = 2
XOFF = 32
XTW = XOFF + PASS
IN_COLS = 9 * D
LN_EPS = 1e-5
DN_ALPHA = 2.0 ** 0.25

SB_BLK = 64


class _Op:
    __slots__ = ("idx", "eng", "fn", "deps", "dma", "sig", "semkey", "value", "prev_value", "inc")

    def __init__(self, idx, eng, fn, dma):
        self.idx = idx
        self.eng = eng
        self.fn = fn
        self.deps = {}
        self.dma = dma
        self.sig = dma
        self.semkey = None
        self.value = 0
        self.prev_value = 0
        self.inc = 16 if dma else 1


class Sched:
    COMPUTE = ("pe", "act", "dve", "pool")

    def __init__(self, sb_bytes, n_dma_sems=12):
        self.ops = []
        self.nsb = (sb_bytes + SB_BLK - 1) // SB_BLK
        self.lw = {"sb": np.full(self.nsb, -1, np.int64), "ps": np.full(8, -1, np.int64)}
        self.lr = {"sb": {}, "ps": {}}
        self.n_dma_sems = n_dma_sems
        self.dma_count = {"pool": 0, "sp": 0, "act": 0}
        self.dma_sem_uses = {}

    @staticmethod
    def footprint(ap):
        space = str(ap.space) if hasattr(ap, "space") else ""
        tname = type(ap.tensor).__name__
        if "PSum" in tname:
            sp = "ps"
        elif "SBTensor" in tname:
            sp = "sb"
        else:
            return None
        dsz = mybir.dt.size(ap.dtype)
        dims = [tuple(x) for x in list(ap.ap)[1:]]
        off = int(ap.offset)
        if not dims:
            starts = np.array([off], np.int64)
            run = 1
        else:
            *outer, (ls, lc) = dims
            starts = np.array([off], np.int64)
            for (s, c) in outer:
                starts = (starts[:, None] + np.arange(c, dtype=np.int64)[None, :] * s).ravel()
            run = (lc - 1) * abs(ls) + 1
        b0 = starts * dsz
        b1 = (starts + run) * dsz - 1
        if sp == "ps":
            blk0 = b0 // 2048
            blk1 = b1 // 2048
        else:
            blk0 = b0 // SB_BLK
            blk1 = b1 // SB_BLK
        n = int((blk1 - blk0).max()) + 1
        blks = (blk0[:, None] + np.arange(n)[None, :])
        blks = np.minimum(blks, blk1[:, None]).ravel()
        return sp, np.unique(blks)

    def add(self, eng, fn, reads=(), writes=(), dma=False):
        idx = len(self.ops)
        op = _Op(idx, eng, fn, dma)
        if dma:
            n = self.dma_count[eng]
            self.dma_count[eng] = n + 1
            slot = n % self.n_dma_sems
            op.semkey = "dma_%s_%d" % (eng, slot)
            uses = self.dma_sem_uses.get(op.semkey, 0)
            op.prev_value = 16 * uses
            op.value = 16 * (uses + 1)
            self.dma_sem_uses[op.semkey] = uses + 1
            rkey = op.semkey
        else:
            op.semkey = eng
            rkey = eng
        rfp = [f for f in (self.footprint(a) for a in reads) if f is not None]
        wfp = [f for f in (self.footprint(a) for a in writes) if f is not None]
        deps = op.deps
        for sp, blks in rfp:
            for w in np.unique(self.lw[sp][blks]):
                if w >= 0:
                    deps[int(w)] = True
            if sp == "ps":
                for key, arr in self.lr[sp].items():
                    if key != rkey:
                        for r in np.unique(arr[blks]):
                            if r >= 0:
                                deps.setdefault(int(r), False)
        for sp, blks in wfp:
            for w in np.unique(self.lw[sp][blks]):
                if w >= 0:
                    deps.setdefault(int(w), False)
            for key, arr in self.lr[sp].items():
                for r in np.unique(arr[blks]):
                    if r >= 0:
                        deps.setdefault(int(r), False)
        for sp, blks in rfp:
            arr = self.lr[sp].get(rkey)
            if arr is None:
                arr = np.full(len(self.lw[sp]), -1, np.int64)
                self.lr[sp][rkey] = arr
            arr[blks] = idx
        for sp, blks in wfp:
            self.lw[sp][blks] = idx
            for arr in self.lr[sp].values():
                arr[blks] = -1
        deps.pop(idx, None)
        self.ops.append(op)
        return op

    def _needs_wait(self, x, y, raw):
        if y.dma:
            return True
        if y.eng == x.eng and not x.dma:
            if x.eng == "pe":
                return False
            return raw
        return True

    def resolve(self):
        ops = self.ops

        def ltime(y):
            return y.value if y.dma else y.idx + 1

        known = {e: {} for e in ("pe", "act", "dve", "pool", "sp")}
        snaps = [None] * len(ops)
        plan = {e: [] for e in known}
        for x in ops:
            kn = known[x.eng]
            prods = []
            for yi, raw in x.deps.items():
                y = ops[yi]
                if self._needs_wait(x, y, raw):
                    prods.append(y)
            waits = []
            for y in sorted(prods, key=lambda y: -y.idx):
                if kn.get(y.semkey, 0) >= ltime(y):
                    continue
                waits.append(y)
                y.sig = True
                kn[y.semkey] = ltime(y)
                sn = snaps[y.idx]
                if sn:
                    for k, v in sn.items():
                        if kn.get(k, 0) < v:
                            kn[k] = v
            extra = None
            if x.dma and x.prev_value > 0 and kn.get(x.semkey, 0) < x.prev_value:
                extra = (x.semkey, x.prev_value)
                kn[x.semkey] = x.prev_value
            sn = dict(kn)
            sn[x.semkey] = max(sn.get(x.semkey, 0), ltime(x))
            snaps[x.idx] = sn
            plan[x.eng].append((waits, extra, x))
        counters = {e: 0 for e in self.COMPUTE}
        for x in ops:
            if not x.dma and x.sig:
                counters[x.eng] += 1
                x.value = counters[x.eng]
        out = {}
        for e, lst in plan.items():
            o = []
            for waits, extra, x in lst:
                w = {}
                for y in waits:
                    if w.get(y.semkey, 0) < y.value:
                        w[y.semkey] = y.value
                if extra is not None and w.get(extra[0], 0) < extra[1]:
                    w[extra[0]] = extra[1]
                o.append((list(w.items()), x))
            out[e] = o
        self.plan = out
        return out


def build_program():
    nc = bass.Bass("TRN2", target_bir_lowering=False)

    xT_d = nc.dram_tensor("xT", [D, HALO + TOK], F32, kind="ExternalInput").ap()
    xtok_d = nc.dram_tensor("xtok", [TOK, D], F32, kind="ExternalInput").ap()
    w_in_d = nc.dram_tensor("w_in", [D, IN_COLS], F32, kind="ExternalInput").ap()
    w_oa_d = nc.dram_tensor("w_oa", [D, D], F32, kind="ExternalInput").ap()
    w_ob_d = nc.dram_tensor("w_ob", [D, D], F32, kind="ExternalInput").ap()
    w_out_d = nc.dram_tensor("w_out", [D, D], F32, kind="ExternalInput").ap()
    wsT_d = nc.dram_tensor("wsT", [128, 8 * 128], F32, kind="ExternalInput").ap()
    bs_t = nc.dram_tensor("bs", [8, 128], F32, kind="ExternalInput")
    pp_d = nc.dram_tensor("pp", [128, 128], F32, kind="ExternalInput").ap()
    lng_t = nc.dram_tensor("lng", [1, D], F32, kind="ExternalInput")
    lnb_t = nc.dram_tensor("lnb", [1, D], F32, kind="ExternalInput")
    out_d = nc.dram_tensor("out", [TOK, D], F32, kind="ExternalOutput").ap()

    XT_O = 0
    XT_B = KC * XTW * 2
    YAB_O = XT_O + XT_B
    YA_B = KC * PASS * 2
    MG_O = YAB_O + 2 * YA_B
    WR_O = MG_O + YA_B
    NSLAB = 4
    SLAB_B = KC * 256 * 2
    TR_O = WR_O + NSLAB * SLAB_B
    TR_B = 30720
    CN_O = TR_O + TR_B
    C_O = CN_O
    WSB_O = C_O + 8192
    PP_O = WSB_O + 2048
    ST_O = PP_O + 512
    ARENA = ST_O + 2048
    assert ARENA <= 212000, ARENA

    arena = nc.alloc_sbuf_tensor("arena", [128, ARENA], U8)
    ps = nc.alloc_psum_tensor("ps", [128, 8, 512], F32)

    def view(off, nbytes, dt, pattern=None, **kw):
        v = arena[:, off:off + nbytes].bitcast(dt)
        if pattern is not None:
            v = v.rearrange(pattern, **kw)
        return v

    XT = view(XT_O, XT_B, BF16, "p (k t) -> p k t", k=KC)
    YA = view(YAB_O, YA_B, BF16, "p (k t) -> p k t", k=KC)
    YB = view(YAB_O + YA_B, YA_B, BF16, "p (k t) -> p k t", k=KC)
    WOUT = view(YAB_O, 2 * YA_B, BF16, "p (k n) -> p k n", k=KC)
    MG = view(MG_O, YA_B, BF16, "p (k t) -> p k t", k=KC)
    SLABS = [view(WR_O + i * SLAB_B, SLAB_B, BF16, "p (k c) -> p k c", k=KC) for i in range(NSLAB)]
    CC = view(C_O, 8192, F32, "p (k t) -> p k t", k=16)
    WSB = view(WSB_O, 2048, BF16, "p (h t) -> p h t", h=8)
    PP = view(PP_O, 512, F32)
    STAT = view(ST_O, 2048, F32)
    HSAVE = STAT[:, 448:480].rearrange("p (g t) -> p g t", g=16)

    def pp_lnvg(dc): return PP[:, dc:dc + 1]
    def pp_lnvb(dc): return PP[:, 16 + dc:16 + dc + 1]
    def pp_convw(k, g): return PP[:, 32 + k * 16 + g:32 + k * 16 + g + 1]
    def pp_convb(g): return PP[:, 80 + g:80 + g + 1]
    def pp_bgate(i): return PP[:, 96 + i:96 + i + 1]

    WA = view(XT_O, 32768, BF16, "p (k n) -> p k n", k=KC)
    WB = view(YAB_O, 32768, BF16, "p (k n) -> p k n", k=KC)
    YH = [view(YAB_O + YA_B + i * 4096, 4096, F32) for i in range(NCH)]
    Y2 = [view(WR_O + 2 * SLAB_B + i * 4096, 4096, F32) for i in range(4)]

    A_GV = [view(TR_O + i * 1024, 1024, F32) for i in range(8)]
    A_VH = [view(TR_O + 8192 + i * 512, 512, BF16) for i in range(8)]
    A_T = [view(TR_O + 12288 + i * 2048, 2048, F32) for i in range(4)]
    A_SZ = [view(TR_O + 20480 + i * 2048, 2048, F32) for i in range(2)]
    A_MX = [view(TR_O + 24576 + i * 2048, 2048, F32) for i in range(2)]
    B_XB = [view(TR_O + i * 2112, 2112, F32) for i in range(4)]
    B_CV = [view(TR_O + 8448 + i * 2048, 2048, F32) for i in range(4)]
    B_SZ = [view(TR_O + 16640 + i * 2048, 2048, F32) for i in range(2)]
    C_GA = [view(TR_O + i * 2048, 2048, F32) for i in range(4)]
    C_GB = [view(TR_O + 8192 + i * 2048, 2048, F32) for i in range(4)]
    C_GA2 = [view(TR_O + 16384 + i * 2048, 2048, F32) for i in range(4)]
    LNG = view(TR_O + 14336, 8192, F32)
    LNB = view(TR_O + 22528, 8192, F32)
    S_WSF = view(TR_O + 20480, 4096, F32, "p (h t) -> p h t", h=8)
    S_WSF2 = view(TR_O + 20480, 4096, F32)
    S_BS = view(TR_O + 24576, 4096, F32, "p (h t) -> p h t", h=8)
    S_ONE = view(TR_O + 28672, 512, F32)

    S = Sched(ARENA)

    def PE(fn, reads, writes): return S.add("pe", fn, reads, writes)
    def ACT(fn, reads, writes): return S.add("act", fn, reads, writes)
    def DVE(fn, reads, writes): return S.add("dve", fn, reads, writes)
    def POOL(fn, reads, writes): return S.add("pool", fn, reads, writes)
    def DMA(q, out, in_):
        return S.add(q, lambda e, o=out, i=in_: e.dma_start(out=o, in_=i), [in_], [out], dma=True)

    def mm_group(out, pairs):
        n = len(pairs)
        for i, (l, r) in enumerate(pairs):
            PE(lambda e, o=out, l=l, r=r, i=i, n=n: e.matmul(o, l, r, start=(i == 0), stop=(i == n - 1)),
               [l, r], [out])

    def act(out, in_, func, bias=None, scale=None):
        kw = {}
        rd = [in_]
        if bias is not None:
            kw["bias"] = bias
            rd.append(bias)
        if scale is not None:
            kw["scale"] = scale
            if not isinstance(scale, float):
                rd.append(scale)
        ACT(lambda e, o=out, i=in_, f=func, kw=kw: e.activation(out=o, in_=i, func=f, **kw), rd, [out])

    def tt(out, a, b, op, eng=None):
        (eng or DVE)(lambda e, o=out, a=a, b=b, op=op: e.tensor_tensor(out=o, in0=a, in1=b, op=op), [a, b], [out])

    def ts(out, a, s1, s2, op0, op1=None):
        rd = [a] + [s for s in (s1, s2) if s is not None and not isinstance(s, float)]
        if op1 is None:
            DVE(lambda e, o=out, a=a, s1=s1, op0=op0: e.tensor_scalar(out=o, in0=a, scalar1=s1, scalar2=None, op0=op0),
                rd, [out])
        else:
            DVE(lambda e, o=out, a=a, s1=s1, s2=s2, op0=op0, op1=op1:
                e.tensor_scalar(out=o, in0=a, scalar1=s1, scalar2=s2, op0=op0, op1=op1), rd, [out])

    def stt(out, a, s, b, op0, op1):
        rd = [a, b] + ([] if isinstance(s, float) else [s])
        DVE(lambda e, o=out, a=a, s=s, b=b, op0=op0, op1=op1:
            e.scalar_tensor_tensor(out=o, in0=a, scalar=s, in1=b, op0=op0, op1=op1), rd, [out])

    slab_reqs = []
    for _p in range(NPASS):
        for h in range(8):
            slab_reqs += [(w_in_d, 1 * D + h * 256), (w_in_d, 0 * D + h * 256), (w_in_d, 2 * D + h * 256)]
        for jj in range(8):
            slab_reqs += [(w_in_d, (3 + b_) * D + jj * 256) for b_ in range(4)]
        for ee in range(8):
            if ee != 7:
                slab_reqs.append((w_in_d, 7 * D + ee * 256))
            slab_reqs.append((w_in_d, 8 * D + ee * 256))
            if ee == 6:
                slab_reqs.append((w_in_d, 7 * D + 7 * 256))
            slab_reqs += [(w_oa_d, ee * 256), (w_ob_d, ee * 256)]
    SLABS_PER_PASS = len(slab_reqs) // NPASS
    issued = [0]
    limit = [SLABS_PER_PASS]
    taken = [0]

    def issue_upto(i):
        while issued[0] <= min(i, limit[0], len(slab_reqs) - 1):
            j = issued[0]
            src2d, col0 = slab_reqs[j]
            DMA("pool", SLABS[j % NSLAB], src2d[:, col0:col0 + 256].rearrange("(k p) c -> p k c", p=128))
            issued[0] += 1

    def take():
        i = taken[0]
        taken[0] += 1
        assert i <= limit[0]
        issue_upto(i)
        issue_upto(i + NSLAB - 1)
        return SLABS[i % NSLAB]

    def load_xT_cols(p, c0, c1):
        lo = XOFF - HALO if c0 == 0 else XOFF + c0
        slo = p * PASS + (0 if c0 == 0 else HALO + c0)
        DMA("pool", XT[:, :, lo:XOFF + c1],
            xT_d[:, slo:p * PASS + HALO + c1].rearrange("(k p) t -> p k t", p=128))

    def load_xT(p, quarters):
        if p > 0 and list(quarters) == [0, 1, 2, 3]:
            load_xT_cols(p, 0, 512)
            load_xT_cols(p, 512, 1024)
            return
        for q in quarters:
            if p == 0 and q == 0:
                load_xT_cols(p, 0, 128)
                load_xT_cols(p, 128, 256)
            else:
                load_xT_cols(p, q * 256, (q + 1) * 256)

    issue_upto(0)
    load_xT(0, [0, 1, 2, 3])
    issue_upto(1)

    DMA("sp", PP, pp_d)
    DMA("sp", S_WSF, wsT_d.rearrange("p (h t) -> p h t", h=8))
    DMA("sp", S_BS, bass.AP(bs_t, 0, [[0, 128], [128, 8], [1, 128]]))
    NHALF = STAT[:, 128:136]
    DVE(lambda e: e.memset(NHALF, -0.5), [], [NHALF])
    EPS_AP = STAT[:, 144:145]
    DVE(lambda e: e.memset(EPS_AP, LN_EPS), [], [EPS_AP])

    def setup_consts():
        POOL(lambda e: e.affine_select(out=S_WSF, in_=S_WSF, pattern=[[0, 8], [1, 128]],
                                       compare_op=ALU.is_ge, fill=0.0, base=0, channel_multiplier=-1),
             [S_WSF], [S_WSF])
        DVE(lambda e: e.memset(S_ONE, 1.0), [], [S_ONE])
        DVE(lambda e: e.tensor_copy(out=WSB, in_=S_WSF), [S_WSF], [WSB])
        for half in range(2):
            o = ps[:, 2 + half, :]
            r = S_WSF2[:, half * 512:(half + 1) * 512]
            PE(lambda e, o=o, r=r: e.matmul(o, S_ONE, r, start=True, stop=True), [S_ONE, r], [o])
        for dc in range(16):
            h = dc // 2
            rw = ps[:, 2 + h // 4, (h % 4) * 128:(h % 4 + 1) * 128]
            stt(CC[:, dc, :], rw, pp_lnvb(dc), S_BS[:, h, :], ALU.mult, ALU.add)

    def xt_tok(k, t0, n):
        return XT[:, k, XOFF + t0:XOFF + t0 + n]

    for p in range(NPASS):
        for h in range(8):
            wv = take()
            par = h % 2
            MV = STAT[:, 32 + par * 16:32 + par * 16 + 16].rearrange("p (c t) -> p c t", c=8)
            RS = STAT[:, 64 + par * 16:64 + par * 16 + 8]
            for c in range(NCH):
                pv = ps[:, c, 0:256]
                mm_group(pv, [(xt_tok(k, c * 128, 128), wv[:, k, :]) for k in range(KC)])
                gv = A_GV[c]
                act(gv, pv, AF.Gelu_apprx_tanh)
                st6 = STAT[:, (c % 2) * 16:(c % 2) * 16 + 6]
                DVE(lambda e, o=st6, i=gv: e.bn_stats(out=o, in_=i), [gv], [st6])
                DVE(lambda e, o=MV[:, c, :], i=st6: e.bn_aggr(out=o, in_=i), [st6], [MV[:, c, :]])
            VE = STAT[:, 96 + par * 16:96 + par * 16 + 8]
            ts(VE, MV[:, :, 1], LN_EPS, None, ALU.add)
            POOL(lambda e, o=RS, i=VE: e.tensor_tensor(out=o, in0=i, in1=NHALF[:, 0:8], op=ALU.pow), [VE, NHALF[:, 0:8]], [RS])
            for c in range(NCH):
                ts(A_VH[c], A_GV[c], MV[:, c, 0:1], RS[:, c:c + 1], ALU.subtract, ALU.mult)
            wu = take()
            for dh in range(2):
                for t in range(NTT):
                    i = dh * 2 + t
                    pu = ps[:, i % 4, :]
                    mm_group(pu, [(wu[:, k, dh * 128:(dh + 1) * 128], xt_tok(k, t * 512, 512)) for k in range(KC)])
                    act(A_T[i], pu, AF.Gelu_apprx_tanh)
            if p == 0 and h == 0:
                setup_consts()
            wz = take()
            for dh in range(2):
                for t in range(NTT):
                    i = dh * 2 + t
                    pz = ps[:, i % 4, :]
                    mm_group(pz, [(wz[:, k, dh * 128:(dh + 1) * 128], xt_tok(k, t * 512, 512)) for k in range(KC)])
                    act(A_SZ[i % 2], pz, AF.Silu)
                    tt(A_T[i], A_T[i], A_SZ[i % 2], ALU.mult)
            for dh in range(2):
                for t in range(NTT):
                    i = dh * 2 + t
                    dc = h * 2 + dh
                    pm = ps[:, 4 + i, :]
                    for c4 in range(4):
                        o = pm[:, c4 * 128:(c4 + 1) * 128]
                        l = A_VH[t * 4 + c4][:, dh * 128:(dh + 1) * 128]
                        r = WSB[:, h, :]
                        PE(lambda e, o=o, l=l, r=r: e.matmul(o, l, r, start=True, stop=True), [l, r], [o])
                    mx = A_MX[i % 2]
                    stt(mx.rearrange("p (c t) -> p c t", c=4), pm.rearrange("p (c t) -> p c t", c=4),
                        pp_lnvg(dc), CC[:, dc, :].unsqueeze(1).broadcast_to([128, 4, 128]), ALU.mult, ALU.add)
                    tt(YA[:, dc, t * 512:(t + 1) * 512], mx, A_T[i], ALU.mult)

        for jj in range(8):
            ring = [0]

            def nxt():
                b = 1 + ring[0] % 7
                ring[0] += 1
                return ps[:, b, :]
            wxb = take()
            for j2 in range(2):
                ws = wxb[:, :, j2 * 128:(j2 + 1) * 128]
                if p == 0:
                    ph = ps[:, 0, j2 * 2:j2 * 2 + 2]
                    mm_group(ph, [(ws[:, k, :], XT[:, k, XOFF - HALO:XOFF]) for k in range(KC)])
                    act(B_XB[j2 * 2][:, 0:2], ph, AF.Copy)
                else:
                    DVE(lambda e, o=B_XB[j2 * 2][:, 0:2], i=HSAVE[:, jj * 2 + j2, :]: e.tensor_copy(out=o, in_=i),
                        [HSAVE[:, jj * 2 + j2, :]], [B_XB[j2 * 2][:, 0:2]])
                for t in range(NTT):
                    pt = nxt()
                    mm_group(pt, [(ws[:, k, :], xt_tok(k, t * 512, 512)) for k in range(KC)])
                    act(B_XB[j2 * 2 + t][:, 2:514], pt, AF.Copy)
            wcb = take()
            for j2 in range(2):
                g = jj * 2 + j2
                ws = wcb[:, :, j2 * 128:(j2 + 1) * 128]
                if p == 0:
                    ph = ps[:, 0, 4 + j2 * 2:4 + j2 * 2 + 2]
                    mm_group(ph, [(ws[:, k, :], XT[:, k, XOFF - HALO:XOFF]) for k in range(KC)])
                    tt(B_XB[j2 * 2][:, 0:2], B_XB[j2 * 2][:, 0:2], ph, ALU.mult)
                for t in range(NTT):
                    pt = nxt()
                    hx = B_XB[j2 * 2 + t]
                    cv = B_CV[j2 * 2 + t]
                    mm_group(pt, [(ws[:, k, :], xt_tok(k, t * 512, 512)) for k in range(KC)])
                    tt(hx[:, 2:514], hx[:, 2:514], pt, ALU.mult)
                    if t == 1:
                        DVE(lambda e, o=hx[:, 0:2], i=B_XB[j2 * 2][:, 512:514]: e.tensor_copy(out=o, in_=i),
                            [B_XB[j2 * 2][:, 512:514]], [hx[:, 0:2]])
                        if p + 1 < NPASS:
                            DVE(lambda e, o=HSAVE[:, g, :], i=hx[:, 512:514]: e.tensor_copy(out=o, in_=i),
                                [hx[:, 512:514]], [HSAVE[:, g, :]])
                    act(cv, hx[:, 2:514], AF.Identity, bias=pp_convb(g), scale=pp_convw(2, g))
                    stt(cv, hx[:, 1:513], pp_convw(1, g), cv, ALU.mult, ALU.add)
                    stt(cv, hx[:, 0:512], pp_convw(0, g), cv, ALU.mult, ALU.add)
            wbb = take()
            for j2 in range(2):
                ws = wbb[:, :, j2 * 128:(j2 + 1) * 128]
                for t in range(NTT):
                    pt = nxt()
                    cv = B_CV[j2 * 2 + t]
                    mm_group(pt, [(ws[:, k, :], xt_tok(k, t * 512, 512)) for k in range(KC)])
                    tt(cv, cv, pt, ALU.mult)
            wzb = take()
            for j2 in range(2):
                g = jj * 2 + j2
                ws = wzb[:, :, j2 * 128:(j2 + 1) * 128]
                for t in range(NTT):
                    pt = nxt()
                    i = j2 * 2 + t
                    mm_group(pt, [(ws[:, k, :], xt_tok(k, t * 512, 512)) for k in range(KC)])
                    act(B_SZ[i % 2], pt, AF.Silu)
                    tt(YB[:, g, t * 512:(t + 1) * 512], B_CV[i], B_SZ[i % 2], ALU.mult)

        cring = [0]

        def c_nxt():
            b = cring[0] % 8
            cring[0] += 1
            return ps[:, b, :]

        def c_gate(ee, boff, G):
            wsl = take()
            for e2 in range(2):
                e_ = ee * 2 + e2
                for t in range(NTT):
                    pt = c_nxt()
                    mm_group(pt, [(wsl[:, k, e2 * 128:(e2 + 1) * 128], xt_tok(k, t * 512, 512)) for k in range(KC)])
                    act(G[e2 * 2 + t], pt, AF.Sigmoid, bias=pp_bgate(boff + e_))

        def c_proj(ee, G, Y, GA, final):
            wsl = take()
            for e2 in range(2):
                e_ = ee * 2 + e2
                for t in range(NTT):
                    pt = c_nxt()
                    mm_group(pt, [(wsl[:, k, e2 * 128:(e2 + 1) * 128], Y[:, k, t * 512:(t + 1) * 512]) for k in range(KC)])
                    tt(G[e2 * 2 + t], G[e2 * 2 + t], pt, ALU.mult)
                    if final:
                        tt(MG[:, e_, t * 512:(t + 1) * 512], GA[e2 * 2 + t], C_GB[e2 * 2 + t], ALU.add)

        for ee in range(8):
            GAe = C_GA2 if ee == 7 else C_GA
            if ee != 7:
                c_gate(ee, 0, C_GA)
            c_gate(ee, 16, C_GB)
            if ee == 6:
                c_gate(7, 0, C_GA2)
            c_proj(ee, GAe, YA, GAe, False)
            c_proj(ee, C_GB, YB, GAe, True)

        for n in range(2):
            DMA("pool", WA[:, :, n * 512:(n + 1) * 512],
                w_out_d[:, n * 512:(n + 1) * 512].rearrange("(k p) c -> p k c", p=128))

        LASTP = (p == NPASS - 1)

        def d_load_wb():
            for n in range(2):
                DMA("pool", WB[:, :, n * 512:(n + 1) * 512],
                    w_out_d[:, (2 + n) * 512:(3 + n) * 512].rearrange("(k p) c -> p k c", p=128))


        def d_load_yh(c):
            r0 = p * PASS + c * 128
            DMA("sp", YH[c], xtok_d[r0:r0 + 128, 0:1024])

        d_load_yh(0)
        d_load_yh(1)

        def d_stats(c, n):
            return STAT[:, 192 + c * 24 + n * 6:192 + c * 24 + (n + 1) * 6]

        dcnt = [0]

        def d_evac(c, n, dst, W):
            j = n % 2
            po = ps[:, dcnt[0] % 8, :]
            dcnt[0] += 1
            mm_group(po, [(MG[:, k, c * 128:(c + 1) * 128], W[:, k, j * 512:(j + 1) * 512]) for k in range(KC)])
            blk = dst[:, j * 512:(j + 1) * 512]
            stt(blk, blk, float(DN_ALPHA), po, ALU.mult, ALU.add)
            st = d_stats(c, n)
            DVE(lambda e, o=st, i=blk: e.bn_stats(out=o, in_=i), [blk], [st])

        def d_gb(c):
            y2 = Y2[c % 4]
            beng = DVE if (LASTP and c == NCH - 1) else POOL
            tt(YH[c], YH[c], LNG[:, 0:1024], ALU.mult)
            tt(YH[c], YH[c], LNB[:, 0:1024], ALU.add, eng=beng)
            tt(y2, y2, LNG[:, 1024:2048], ALU.mult)
            tt(y2, y2, LNB[:, 1024:2048], ALU.add, eng=beng)

        def d_out(c, trailing=False):
            r0 = p * PASS + c * 128
            q = "sp" if (trailing and (not LASTP or c == NCH - 1)) else "act"
            DMA(q, out_d[r0:r0 + 128, 0:1024], YH[c])
            DMA(q, out_d[r0:r0 + 128, 1024:2048], Y2[c % 4])

        def sweep1(c):
            for n in range(2):
                d_evac(c, n, YH[c], WA)
            if c + 2 < NCH:
                d_load_yh(c + 2)
            if c == 0:
                d_load_wb()
                limit[0] = (p + 1) * SLABS_PER_PASS + 1
                issue_upto((p + 1) * SLABS_PER_PASS + 1)
            if c == 2:
                DMA("sp", LNG, bass.AP(lng_t, 0, [[0, 128], [1, D]]))
                DMA("sp", LNB, bass.AP(lnb_t, 0, [[0, 128], [1, D]]))

        def sweep2(c):
            r0 = p * PASS + c * 128
            y2 = Y2[c % 4]
            DMA("sp", y2, xtok_d[r0:r0 + 128, 1024:2048])
            last = (LASTP and c == NCH - 1)
            if c >= 1 and not last:
                d_gb(c - 1)
            for n in range(2, 4):
                d_evac(c, n, y2, WB)
            so = 384 + (c % 2) * 32
            mv = STAT[:, so:so + 2]
            rstd = STAT[:, so + 4:so + 5]
            nmr = STAT[:, so + 5:so + 6]
            sd = STAT[:, so + 16:so + 17]
            DVE(lambda e, o=mv, i=STAT[:, 192 + c * 24:192 + (c + 1) * 24]: e.bn_aggr(out=o, in_=i),
                [STAT[:, 192 + c * 24:192 + (c + 1) * 24]], [mv])
            if (not LASTP) and c == NCH - 1:
                ve = STAT[:, so + 8:so + 9]
                ts(ve, STAT[:, so + 1:so + 2], LN_EPS, None, ALU.add)
                POOL(lambda e, o=rstd, i=ve: e.tensor_tensor(out=o, in0=i, in1=NHALF[:, 0:1], op=ALU.pow),
                     [ve, NHALF[:, 0:1]], [rstd])
                ts(YH[c], YH[c], STAT[:, so:so + 1], rstd, ALU.subtract, ALU.mult)
                ts(y2, y2, STAT[:, so:so + 1], rstd, ALU.subtract, ALU.mult)
            else:
                act(sd, STAT[:, so + 1:so + 2], AF.Sqrt, bias=EPS_AP)
                DVE(lambda e, o=rstd, i=sd: e.reciprocal(out=o, in_=i), [sd], [rstd])
                stt(nmr, STAT[:, so:so + 1], -1.0, rstd, ALU.mult, ALU.mult)
                act(YH[c], YH[c], AF.Identity, bias=nmr, scale=rstd)
                act(y2, y2, AF.Identity, bias=nmr, scale=rstd)
            if last:
                d_gb(c - 1)
            if c >= 2:
                d_out(c - 2)

        for c in range(NCH):
            sweep1(c)
        if not LASTP:
            load_xT(p + 1, [0, 1, 2, 3])
        for c in range(NCH):
            sweep2(c)
        if not LASTP:
            limit[0] = (p + 1) * SLABS_PER_PASS + 2
            issue_upto((p + 1) * SLABS_PER_PASS + 2)
        d_gb(NCH - 1)
        d_out(NCH - 2, trailing=True)
        d_out(NCH - 1, trailing=True)
        limit[0] = (p + 2) * SLABS_PER_PASS

    plan = S.resolve()

    semkeys = set()
    for x in S.ops:
        if x.sig:
            semkeys.add(x.semkey)
    semkeys = sorted(semkeys)
    sems = {k: nc.alloc_semaphore("s_" + k) for k in semkeys}

    def emit(engine, key):
        for waits, x in plan[key]:
            for k, v in waits:
                engine.wait_ge(sems[k], v)
            ins = x.fn(engine)
            if x.sig:
                ins.then_inc(sems[x.semkey], x.inc)
        if key == "sp":
            for k in semkeys:
                if k.startswith("dma_"):
                    engine.wait_ge(sems[k], 16 * S.dma_sem_uses[k])

    with nc.Block() as block:
        @block.tensor
        def _(e):
            emit(e, "pe")

        @block.scalar
        def _(e):
            emit(e, "act")

        @block.vector
        def _(e):
            emit(e, "dve")

        @block.gpsimd
        def _(e):
            emit(e, "pool")

        @block.sync
        def _(e):
            emit(e, "sp")

    return nc


_NC_CACHE = {}


def _get_program():
    if "nc" not in _NC_CACHE:
        _NC_CACHE["nc"] = build_program()
    return _NC_CACHE["nc"]


def _prepare(x, w_in, b_gate, ln_v_g, ln_v_b, w_s, b_s, conv_w, conv_b, w_oa, w_ob, w_out, ln_g, ln_b,
             cores=range(N_CORES)):
    x = np.asarray(x, dtype=np.float32)
    f = lambda a: np.ascontiguousarray(np.asarray(a, dtype=np.float32))
    w_in0 = f(w_in[0])
    w_oa0 = f(w_oa[0])
    w_ob0 = f(w_ob[0])
    w_out0 = f(w_out[0])
    wsT = f(np.transpose(np.asarray(w_s[0], np.float32), (2, 0, 1)).reshape(128, 8 * 128))
    bs = f(b_s[0])
    pp = np.zeros((128, 128), np.float32)
    pp[:, 0:16] = np.asarray(ln_v_g[0], np.float32).reshape(16, 128).T
    pp[:, 16:32] = np.asarray(ln_v_b[0], np.float32).reshape(16, 128).T
    cw = np.asarray(conv_w[0], np.float32)
    for k in range(3):
        pp[:, 32 + k * 16:48 + k * 16] = cw[k].reshape(16, 128).T
    pp[:, 80:96] = np.asarray(conv_b[0], np.float32).reshape(16, 128).T
    pp[:, 96:128] = np.asarray(b_gate[0], np.float32).reshape(32, 128).T
    lng = f(np.asarray(ln_g[0], np.float32).reshape(1, D))
    lnb = f(np.asarray(ln_b[0], np.float32).reshape(1, D))

    in_maps = []
    for c in cores:
        b = c // (SEQ // TOK)
        s0 = (c % (SEQ // TOK)) * TOK
        xs = x[b, s0:s0 + TOK, :]
        xT = np.zeros((D, HALO + TOK), np.float32)
        xT[:, HALO:] = xs.T
        if s0 > 0:
            xT[:, :HALO] = x[b, s0 - HALO:s0, :].T
        in_maps.append({
            "xT": xT, "xtok": np.ascontiguousarray(xs), "w_in": w_in0, "w_oa": w_oa0, "w_ob": w_ob0,
            "w_out": w_out0, "wsT": wsT, "bs": bs, "pp": pp, "lng": lng, "lnb": lnb,
        })

    return in_maps


def kernel(x, w_in, b_gate, ln_v_g, ln_v_b, w_s, b_s, conv_w, conv_b, w_oa, w_ob, w_out, ln_g, ln_b):
    in_maps = _prepare(x, w_in, b_gate, ln_v_g, ln_v_b, w_s, b_s, conv_w, conv_b, w_oa, w_ob, w_out, ln_g, ln_b)
    nc = _get_program()
    res = run_bass_kernel_spmd(nc, in_maps, core_ids=list(range(N_CORES)))
    out = np.empty((2, SEQ, D), np.float32)
    for c in range(N_CORES):
        b = c // (SEQ // TOK)
        s0 = (c % (SEQ // TOK)) * TOK
        out[b, s0:s0 + TOK, :] = res.results[c]["out"]
    return out
```
